# Optimizing a Trainium2 kernel written in Bass

```python
import math
import jax
import jax.numpy as jnp
from jax import lax
import numpy as np

D_MODEL = 2048
BATCH = 2
SEQ = 16384
DEPTH = 1
DEC_BATCH = 4
DEC_SEQ = 8192
PAST_LEN = 128

N_MEM = 256
CHUNK = 128
A_GROUPS = 8
A_WIDTH = D_MODEL
B_HEADS = 8
B_HEAD_DIM = 128
B_QK_WIDTH = B_HEADS * 2 * B_HEAD_DIM
B_V_WIDTH = B_HEADS * 2 * B_HEAD_DIM
Q_BLOCK = 128
N_BUCKETS = 32
MAX_DISTANCE = 128
C_HEADS = 4
C_HEAD_DIM = 128
C_WIDTH = C_HEADS * C_HEAD_DIM
D_FF = -(-8 * D_MODEL // (3 * 256)) * 256
N_IN = 2 * A_WIDTH + 2 * B_QK_WIDTH + B_V_WIDTH + 2 * D_MODEL
EPS = 1e-6

kernel_name = 'hybrid_gmlp_diffattn_encoder'


def _rmsnorm(x, g, eps=EPS):
    xf = x.astype(jnp.float32)
    y = xf * lax.rsqrt(jnp.mean(xf * xf, axis=-1, keepdims=True) + eps)
    return (y * g.astype(jnp.float32)).astype(x.dtype)


def _layernorm(x, g, b):
    xf = x.astype(jnp.float32)
    xc = xf - jnp.mean(xf, axis=-1, keepdims=True)
    y = xc * lax.rsqrt(jnp.mean(xc * xc, axis=-1, keepdims=True) + EPS)
    return (y * g.astype(jnp.float32) + b.astype(jnp.float32)).astype(x.dtype)


def _t5_bucket(rel):
    half = N_BUCKETS // 2
    max_exact = half // 2
    n = jnp.abs(rel)
    nf = jnp.maximum(n, 1).astype(jnp.float32)
    large = max_exact + (jnp.log(nf / max_exact) / math.log(MAX_DISTANCE / max_exact)
                         * (half - max_exact)).astype(jnp.int32)
    large = jnp.minimum(large, half - 1)
    return jnp.where(rel > 0, half, 0) + jnp.where(n < max_exact, n, large)


def _chunked_spatial_gating(u, v, ln_g, ln_b, w_s, b_s):
    bsz, seq, _ = v.shape
    u = jax.nn.gelu(u)
    v = _layernorm(jax.nn.gelu(v), ln_g, ln_b)
    vc = v.reshape(bsz, seq // CHUNK, CHUNK, A_GROUPS, A_WIDTH // A_GROUPS)
    mixed = jnp.einsum('gij,bcjgd->bcigd', w_s, vc) + jnp.transpose(b_s)[:, :, None]
    return u * mixed.reshape(bsz, seq, A_WIDTH)


def _diff_attention(q, k, v, rel_table, lam, lam_init, subln_g):
    bsz, seq, _ = q.shape
    d = B_HEAD_DIM
    n_blk = seq // Q_BLOCK
    q = q.reshape(bsz, n_blk, Q_BLOCK, B_HEADS, 2, d).transpose(4, 1, 0, 3, 2, 5)
    k = k.reshape(bsz, seq, B_HEADS, 2, d).transpose(3, 0, 2, 1, 4)
    v = v.reshape(bsz, seq, B_HEADS, 2 * d).transpose(0, 2, 1, 3)
    scale = d ** -0.5
    k_pos = jnp.arange(seq, dtype=jnp.int32)

    def one_block(args):
        q1b, q2b, start = args
        q_pos = start + jnp.arange(Q_BLOCK, dtype=jnp.int32)
        bucket = _t5_bucket(k_pos[None, :] - q_pos[:, None])
        bias = jnp.transpose(rel_table[bucket].astype(jnp.float32), (2, 0, 1))
        p1 = jax.nn.softmax(jnp.einsum('bhqd,bhkd->bhqk', q1b, k[0]).astype(jnp.float32) * scale + bias, axis=-1)
        p2 = jax.nn.softmax(jnp.einsum('bhqd,bhkd->bhqk', q2b, k[1]).astype(jnp.float32) * scale + bias, axis=-1)
        a = (p1 - lam * p2).astype(v.dtype)
        return jnp.einsum('bhqk,bhke->bhqe', a, v)

    starts = jnp.arange(n_blk, dtype=jnp.int32) * Q_BLOCK
    o = lax.map(one_block, (q[0], q[1], starts))
    o = _rmsnorm(o, subln_g, eps=1e-5) * (1.0 - lam_init)
    return o.transpose(1, 0, 3, 2, 4).reshape(bsz, seq, B_HEADS * 2 * d)


def _memory_cross_attention(x_n, mem_n, w_cq, w_ck, w_cv, w_co):
    bsz, seq, _ = x_n.shape
    q = (x_n @ w_cq).reshape(bsz, seq, C_HEADS, C_HEAD_DIM)
    k = (mem_n @ w_ck).reshape(bsz, N_MEM, C_HEADS, C_HEAD_DIM)
    v = (mem_n @ w_cv).reshape(bsz, N_MEM, C_HEADS, C_HEAD_DIM)
    s = jnp.einsum('bqhd,bkhd->bhqk', q, k).astype(jnp.float32) * (C_HEAD_DIM ** -0.5)
    p = jax.nn.softmax(s, axis=-1).astype(v.dtype)
    o = jnp.einsum('bhqk,bkhd->bqhd', p, v).reshape(bsz, seq, C_WIDTH)
    return o @ w_co


def _encoder_trunk(x, mem, rel_bias_table, norm_mix_g, w_in, ln_v_g, ln_v_b, w_spatial, b_spatial,
                   lambda_q1, lambda_k1, lambda_q2, lambda_k2, subln_g, w_proj_a, w_proj_b, w_out,
                   norm_cross_g, norm_mem_g, w_cq, w_ck, w_cv, w_co, norm_ffn_g, w_ffn_in, w_ffn_out,
                   norm_final_g):
    splits = [A_WIDTH, 2 * A_WIDTH, 2 * A_WIDTH + B_QK_WIDTH, 2 * A_WIDTH + 2 * B_QK_WIDTH,
              2 * A_WIDTH + 2 * B_QK_WIDTH + B_V_WIDTH,
              2 * A_WIDTH + 2 * B_QK_WIDTH + B_V_WIDTH + D_MODEL]
    for l in range(DEPTH):
        lam_init = 0.8 - 0.6 * math.exp(-0.3 * l)
        h = _rmsnorm(x, norm_mix_g[l])
        u, va, qb, kb, vb, g_a, g_b = jnp.split(h @ w_in[l], splits, axis=-1)
        o_a = _chunked_spatial_gating(u, va, ln_v_g[l], ln_v_b[l], w_spatial[l], b_spatial[l])
        lam = (jnp.exp(jnp.sum(lambda_q1[l].astype(jnp.float32) * lambda_k1[l].astype(jnp.float32)))
               - jnp.exp(jnp.sum(lambda_q2[l].astype(jnp.float32) * lambda_k2[l].astype(jnp.float32)))
               + lam_init)
        o_b = _diff_attention(qb, kb, vb, rel_bias_table, lam, lam_init, subln_g[l])
        merged = jax.nn.sigmoid(g_a) * (o_a @ w_proj_a[l]) + jax.nn.sigmoid(g_b) * (o_b @ w_proj_b[l])
        x = x + merged @ w_out[l]
        x = x + _memory_cross_attention(_rmsnorm(x, norm_cross_g[l]), _rmsnorm(mem, norm_mem_g[l]),
                                        w_cq[l], w_ck[l], w_cv[l], w_co[l])
        gate, up = jnp.split(_rmsnorm(x, norm_ffn_g[l]) @ w_ffn_in[l], 2, axis=-1)
        x = x + (jax.nn.silu(gate) * up) @ w_ffn_out[l]
    return _rmsnorm(x, norm_final_g)


def setup_inputs(seed: int = 0) -> dict:
    key = jax.random.key(seed)
    ks = jax.random.split(key, 32)

    def nrm(k, shape, scale):
        return jax.random.normal(k, shape, dtype=jnp.float32) * scale

    def gain(k, shape):
        return 1.0 + nrm(k, shape, 0.02)

    L = DEPTH
    return {
        'x_prompt': nrm(ks[0], (BATCH, SEQ, D_MODEL), 1.0),
        'x_sample': nrm(ks[1], (DEC_BATCH, DEC_SEQ, D_MODEL), 1.0),
        'mem_prompt': nrm(ks[2], (BATCH, N_MEM, D_MODEL), 1.0),
        'mem_sample': nrm(ks[3], (DEC_BATCH, N_MEM, D_MODEL), 1.0),
        'rel_bias_table': nrm(ks[4], (N_BUCKETS, B_HEADS), 0.5),
        'norm_mix_g': gain(ks[5], (L, D_MODEL)),
        'w_in': nrm(ks[6], (L, D_MODEL, N_IN), D_MODEL ** -0.5),
        'ln_v_g': gain(ks[7], (L, A_WIDTH)),
        'ln_v_b': nrm(ks[8], (L, A_WIDTH), 0.02),
        'w_spatial': nrm(ks[9], (L, A_GROUPS, CHUNK, CHUNK), CHUNK ** -0.5),
        'b_spatial': nrm(ks[10], (L, A_GROUPS, CHUNK), 0.02),
        'lambda_q1': nrm(ks[11], (L, B_HEAD_DIM), 0.1),
        'lambda_k1': nrm(ks[12], (L, B_HEAD_DIM), 0.1),
        'lambda_q2': nrm(ks[13], (L, B_HEAD_DIM), 0.1),
        'lambda_k2': nrm(ks[14], (L, B_HEAD_DIM), 0.1),
        'subln_g': gain(ks[15], (L, 2 * B_HEAD_DIM)),
        'w_proj_a': nrm(ks[16], (L, A_WIDTH, D_MODEL), A_WIDTH ** -0.5),
        'w_proj_b': nrm(ks[17], (L, B_V_WIDTH, D_MODEL), B_V_WIDTH ** -0.5),
        'w_out': nrm(ks[18], (L, D_MODEL, D_MODEL), D_MODEL ** -0.5),
        'norm_cross_g': gain(ks[19], (L, D_MODEL)),
        'norm_mem_g': gain(ks[20], (L, D_MODEL)),
        'w_cq': nrm(ks[21], (L, D_MODEL, C_WIDTH), D_MODEL ** -0.5),
        'w_ck': nrm(ks[22], (L, D_MODEL, C_WIDTH), D_MODEL ** -0.5),
        'w_cv': nrm(ks[23], (L, D_MODEL, C_WIDTH), D_MODEL ** -0.5),
        'w_co': nrm(ks[24], (L, C_WIDTH, D_MODEL), C_WIDTH ** -0.5),
        'norm_ffn_g': gain(ks[25], (L, D_MODEL)),
        'w_ffn_in': nrm(ks[26], (L, D_MODEL, 2 * D_FF), D_MODEL ** -0.5),
        'w_ffn_out': nrm(ks[27], (L, D_FF, D_MODEL), D_FF ** -0.5),
        'norm_final_g': gain(ks[28], (D_MODEL,)),
    }


def reference(x_prompt, x_sample, mem_prompt, mem_sample, rel_bias_table, norm_mix_g, w_in, ln_v_g, ln_v_b,
              w_spatial, b_spatial, lambda_q1, lambda_k1, lambda_q2, lambda_k2, subln_g, w_proj_a, w_proj_b,
              w_out, norm_cross_g, norm_mem_g, w_cq, w_ck, w_cv, w_co, norm_ffn_g, w_ffn_in, w_ffn_out,
              norm_final_g):
    weights = (rel_bias_table, norm_mix_g, w_in, ln_v_g, ln_v_b, w_spatial, b_spatial,
               lambda_q1, lambda_k1, lambda_q2, lambda_k2, subln_g, w_proj_a, w_proj_b, w_out,
               norm_cross_g, norm_mem_g, w_cq, w_ck, w_cv, w_co, norm_ffn_g, w_ffn_in, w_ffn_out,
               norm_final_g)
    y_prompt = _encoder_trunk(x_prompt, mem_prompt, *weights)
    y_sample = _encoder_trunk(x_sample, mem_sample, *weights)
    return (y_prompt, y_sample)
```

```python
import math
import numpy as np
from contextlib import ExitStack
import concourse.bass as bass
import concourse.mybir as mybir
from concourse.bass_utils import run_bass_kernel_spmd

F32 = mybir.dt.float32
BF16 = mybir.dt.bfloat16
AF = mybir.ActivationFunctionType
ALU = mybir.AluOpType
AX = mybir.AxisListType

COMPUTE = ("pe", "act", "dve", "pool")
NDSEM = 16

D = 2048
KC = 16
NIN = 14336
H = 8
DFF = 5632
FC = 44
NMEM = 256
CW = 512
EPS = 1e-6
LAM_INIT = 0.8 - 0.6 * math.exp(-0.3 * 0)
NEG = -30000.0


import types


def _freeze(fn):
    if fn is None or fn.__closure__ is None:
        return fn
    cells = []
    for c in fn.__closure__:
        try:
            v = c.cell_contents
            if isinstance(v, types.FunctionType):
                v = _freeze(v)
            cells.append(types.CellType(v))
        except ValueError:
            cells.append(c)
    return types.FunctionType(fn.__code__, fn.__globals__, fn.__name__, fn.__defaults__, tuple(cells))


class Op:
    __slots__ = ("eng", "fn", "deps", "signal", "sigval", "dslot", "dtarget", "ndma")

    def __init__(self, eng, fn):
        self.eng = eng
        self.fn = fn
        self.deps = []
        self.signal = False
        self.sigval = 0
        self.dslot = None
        self.dtarget = 0
        self.ndma = 1


class Sched:
    def __init__(self, nc):
        self.nc = nc
        self.engs = {"pe": nc.tensor, "act": nc.scalar, "dve": nc.vector, "pool": nc.gpsimd,
                     "sp": nc.sync, "pq": nc.gpsimd}
        self.stream = {"pe": "pe", "act": "act", "dve": "dve", "pool": "pool", "sp": "sp", "pq": "pool"}
        self.ops = []
        self.last_w = {}
        self.readers = {}
        self.waited = {}
        self.waited_d = {}
        self.idx = {}
        self.dma_count = {"sp": 0, "pq": 0}
        self.dma_slot_last = {}
        self.last_by_eng = {}

    def _reduce(self, op, st, deps):
        best = {}
        for d in deps:
            if d is op:
                continue
            de = d.eng
            if de in self.dma_count:
                if (st, id(d)) in self.waited_d:
                    continue
                best[("d", id(d))] = d
            else:
                if de == "pe" and st == "pe":
                    continue
                di = self.idx[id(d)]
                if self.waited.get((st, de), -1) >= di:
                    continue
                cur = best.get(("c", de))
                if cur is None or self.idx[id(cur)] < di:
                    best[("c", de)] = d
        for k, d in best.items():
            d.signal = True
            op.deps.append(d)
            if k[0] == "d":
                self.waited_d[(st, id(d))] = True
            else:
                self.waited[(st, d.eng)] = self.idx[id(d)]

    def add(self, eng, fn, reads=(), writes=(), ndma=1):
        op = Op(eng, _freeze(fn))
        op.ndma = ndma
        self.idx[id(op)] = len(self.ops)
        deps = []
        for k in reads:
            w = self.last_w.get(k)
            if w is not None:
                deps.append(w)
        for k in writes:
            w = self.last_w.get(k)
            if w is not None:
                deps.append(w)
            deps.extend(self.readers.get(k, {}).values())
        st = self.stream[eng]
        if eng in self.dma_count:
            c = self.dma_count[eng]
            self.dma_count[eng] = c + 1
            slot = (eng, c % NDSEM)
            prev = self.dma_slot_last.get(slot)
            if prev is not None:
                deps.append(prev)
            self.dma_slot_last[slot] = op
            op.dslot = slot
            op.dtarget = (prev.dtarget if prev is not None else 0) + 16 * ndma
        self._reduce(op, st, deps)
        self.ops.append(op)
        self.last_by_eng[eng] = op
        for k in reads:
            self.readers.setdefault(k, {})[st] = op
        for k in writes:
            self.last_w[k] = op
            self.readers[k] = {}
        return op

    def barrier(self, streams=("pe", "act", "dve", "pool", "sp")):
        prods = [d for e, d in self.last_by_eng.items() if e in COMPUTE]
        prods += list(self.dma_slot_last.values())
        for st in streams:
            op = Op(st, None)
            self.idx[id(op)] = len(self.ops)
            self._reduce(op, st, prods)
            self.ops.append(op)

    def build(self, sem):
        counts = {e: 0 for e in COMPUTE}
        for op in self.ops:
            if op.eng in COMPUTE and op.signal:
                counts[op.eng] += 1
                op.sigval = counts[op.eng]
        for op in self.ops:
            seng = self.engs[self.stream[op.eng]]
            for d in op.deps:
                if d.eng in COMPUTE:
                    seng.wait_ge(sem[d.eng], d.sigval)
                else:
                    seng.wait_ge(sem[d.dslot], d.dtarget)
            if op.fn is None:
                continue
            res = op.fn(self.engs[op.eng])
            if op.eng in COMPUTE:
                if op.signal:
                    res.then_inc(sem[op.eng], 1)
            else:
                insts = res if isinstance(res, (list, tuple)) else [res]
                assert len(insts) == op.ndma
                for ins in insts:
                    ins.then_inc(sem[op.dslot], 16)


def dap(t, off, dims):
    return bass.AP(t, off, [list(x) for x in dims])


def build_program(TOWN):
    TK = 2 * TOWN
    NOB = TOWN // 128
    NKB = TK // 128
    QC = min(512, TOWN)
    NQC = TOWN // QC
    QS = QC // 128
    U0 = QC + 255
    LU = ((2 * QC + 511 + 511) // 512) * 512
    T2 = min(1024, TOWN)
    T3 = min(256, TOWN)
    T7 = min(2048, TOWN)
    T8 = min(1024, TOWN)

    nc = bass.Bass("TRN2", target_bir_lowering=False)

    def din(name, shape, dt=F32):
        return nc.dram_tensor(name, list(shape), dt, kind="ExternalInput")

    def dscr(name, shape, dt):
        return nc.dram_tensor(name, list(shape), dt, kind="Internal")

    x_own = din("x_own", [TOWN, D]); x_oth = din("x_oth", [TOWN, D]); mem = din("mem", [NMEM, D])
    rel_tab = din("rel_bias_table", [32, H])
    g_mix = din("norm_mix_g", [D]); w_in = din("w_in", [D, NIN])
    ln_g = din("ln_v_g", [D]); ln_b = din("ln_v_b", [D])
    w_sp = din("w_spatial", [8, 128, 128]); b_sp = din("b_spatial", [8, 128])
    lq1 = din("lambda_q1", [128]); lk1 = din("lambda_k1", [128]); lq2 = din("lambda_q2", [128]); lk2 = din("lambda_k2", [128])
    sub_g = din("subln_g", [256])
    w_pa = din("w_proj_a", [D, D]); w_pb = din("w_proj_b", [D, D]); w_out = din("w_out", [D, D])
    g_cross = din("norm_cross_g", [D]); g_mem = din("norm_mem_g", [D])
    w_cq = din("w_cq", [D, 512]); w_ck = din("w_ck", [D, 512]); w_cv = din("w_cv", [D, 512]); w_co = din("w_co", [512, D])
    g_ffn = din("norm_ffn_g", [D]); w_fi = din("w_ffn_in", [D, 2 * DFF]); w_fo = din("w_ffn_out", [DFF, D])
    g_fin = din("norm_final_g", [D])
    oh_all = din("oh_all", [33, 3 * LU])
    sel_all = din("sel_all", [33, 3 * 128])
    y_out = nc.dram_tensor("y", [TOWN, D], F32, kind="ExternalOutput")

    QT = dscr("QT", [H * 2, 128, TOWN], BF16)
    KT = dscr("KT", [H * 2, 128, TK], BF16)
    VW = 288
    VA = dscr("VA", [H, 128, NKB, VW], BF16)
    GU = dscr("GU", [TOWN, D], BF16); GVA = dscr("GVA", [TOWN, D], BF16)
    SGA = dscr("SGA", [KC, 128, TOWN], BF16); SGB = dscr("SGB", [KC, 128, TOWN], BF16)
    A1T = dscr("A1T", [KC, 128, TOWN], BF16); MT = dscr("MT", [KC, 128, TOWN], BF16)
    OB = dscr("OB", [TOWN, D], BF16)
    X1 = dscr("X1", [TOWN, D], F32); X2 = dscr("X2", [TOWN, D], F32)
    UD = dscr("UD", [3, H, LU], F32)
    ACTD = dscr("ACTD", [FC, 128, TOWN], BF16)
    WPB_B = dscr("WPB_B", [D, D], BF16); WOUT_B = dscr("WOUT_B", [D, D], BF16)
    WCQ_B = dscr("WCQ_B", [D, 512], BF16); WCO_B = dscr("WCO_B", [512, D], BF16)
    WFI_B = dscr("WFI_B", [D, 2 * DFF], BF16); WFO_B = dscr("WFO_B", [DFF, D], BF16)

    S = Sched(nc)
    top = ExitStack()
    sem = {e: top.enter_context(nc.semaphore(f"s_{e}")) for e in COMPUTE}
    for q in ("sp", "pq"):
        for i in range(NDSEM):
            sem[(q, i)] = top.enter_context(nc.semaphore(f"d_{q}{i}"))

    def SB(es, name, shape, dt):
        return es.enter_context(nc.sbuf_tensor(name, list(shape), dt))

    def PS(es, name, shape, dt=F32):
        return es.enter_context(nc.psum_tensor(name, list(shape), dt))

    ident = SB(top, "ident", [128, 128], BF16)
    jrev = SB(top, "jrev", [128, 128], BF16)
    onesb = SB(top, "onesb", [128, 128], BF16)
    cls = SB(top, "cls", [128, 4, H], F32)
    neglam = SB(top, "neglam", [128, 1], F32)
    gsub = SB(top, "gsub", [128, 256], F32)
    kct = SB(top, "kct", [128, 4, NMEM], BF16)
    vc = SB(top, "vc", [128, 2, 512], BF16)
    epsc = SB(top, "epsc", [128, 2], F32)

    def bcast_load(es_tile, src, n, key):
        S.add("sp", lambda e: e.dma_start(out=es_tile[:], in_=dap(src, 0, [[0, 128], [1, n]])), writes=[key])

    def rms_block(xs_ap, xkey, gt, gkey, hb, hkey, junk, ss, rstd, tag, eps_col=0):
        S.add("act", lambda e: e.activation(out=junk[:], in_=xs_ap, func=AF.Square, accum_out=ss[:]),
              reads=[xkey], writes=[("junk", tag), ("ss", tag)])
        S.add("act", lambda e: e.activation(out=rstd[:], in_=ss[:], func=AF.Sqrt, bias=epsc[:, eps_col:eps_col + 1], scale=1.0 / D),
              reads=[("ss", tag), "epsc"], writes=[("rstd", tag)])
        S.add("dve", lambda e: e.reciprocal(rstd[:], rstd[:]), reads=[("rstd", tag)], writes=[("rstd", tag)])
        S.add("dve", lambda e: e.scalar_tensor_tensor(out=hb[:], in0=xs_ap, scalar=rstd[:, 0:1], in1=gt[:], op0=ALU.mult, op1=ALU.mult),
              reads=[xkey, ("rstd", tag), gkey], writes=[hkey])

    tcount = [0]

    def transpose_block(hb, hkey, ptr, dst, dkey, tok0, nch=KC):
        slot = tcount[0] % len(ptr)
        tcount[0] += 1
        p = ptr[slot]
        pk = ("ptr", slot)
        for kc in range(nch):
            S.add("pe", lambda e, kc=kc: e.transpose(p[:, kc * 128:(kc + 1) * 128], hb[:, kc * 128:(kc + 1) * 128], ident[:]),
                  reads=[hkey, "ident"], writes=[pk])
        eng = "act" if tcount[0] % 2 == 0 else "dve"
        src = p[:, 0:nch * 128].rearrange("p (k t) -> p k t", k=nch)
        if eng == "act":
            S.add("act", lambda e: e.copy(out=dst[:, 0:nch, tok0:tok0 + 128], in_=src), reads=[pk], writes=[dkey])
        else:
            S.add("dve", lambda e: e.tensor_copy(dst[:, 0:nch, tok0:tok0 + 128], src), reads=[pk], writes=[dkey])

    with ExitStack() as es:
        idf = SB(es, "idf", [128, 128], F32)
        S.add("pool", lambda e: e.memset(idf[:], 0.0), writes=["idf"])
        S.add("pool", lambda e: e.affine_select(out=idf[:], in_=idf[:], pattern=[[-1, 128]], compare_op=ALU.not_equal,
                                                fill=1.0, base=0, channel_multiplier=1), reads=["idf"], writes=["idf"])
        S.add("dve", lambda e: e.tensor_copy(ident[:], idf[:]), reads=["idf"], writes=["ident"])
        jf = SB(es, "jf", [128, 128], F32)
        S.add("pool", lambda e: e.memset(jf[:], 0.0), writes=["jf"])
        S.add("pool", lambda e: e.affine_select(out=jf[:], in_=jf[:], pattern=[[1, 128]], compare_op=ALU.not_equal,
                                                fill=1.0, base=-127, channel_multiplier=1), reads=["jf"], writes=["jf"])
        S.add("dve", lambda e: e.tensor_copy(jrev[:], jf[:]), reads=["jf"], writes=["jrev"])
        S.add("dve", lambda e: e.memset(onesb[:], 1.0), writes=["onesb"])
        S.add("dve", lambda e: e.memset(epsc[:, 0:1], EPS), writes=["epsc"])
        S.add("dve", lambda e: e.memset(epsc[:, 1:2], 1e-5), reads=["epsc"], writes=["epsc"])

        lv = SB(es, "lv", [128, 4, 128], F32)
        for i, t in enumerate((lq1, lk1, lq2, lk2)):
            S.add("sp", lambda e, i=i, t=t: e.dma_start(out=lv[:, i, :], in_=dap(t, 0, [[0, 128], [1, 128]])), writes=[("lv", i)])
        lp = SB(es, "lp", [128, 2, 128], F32)
        ls = SB(es, "ls", [128, 2], F32)
        for j in range(2):
            S.add("dve", lambda e, j=j: e.tensor_tensor(out=lp[:, j, :], in0=lv[:, 2 * j, :], in1=lv[:, 2 * j + 1, :], op=ALU.mult),
                  reads=[("lv", 2 * j), ("lv", 2 * j + 1)], writes=[("lp", j)])
            S.add("dve", lambda e, j=j: e.reduce_sum(out=ls[:, j:j + 1], in_=lp[:, j, :], axis=AX.X), reads=[("lp", j)], writes=[("ls", j)])
        le = SB(es, "le", [128, 2], F32)
        S.add("act", lambda e: e.activation(out=le[:], in_=ls[:], func=AF.Exp), reads=[("ls", 0), ("ls", 1)], writes=["le"])
        S.add("dve", lambda e: e.tensor_tensor(out=neglam[:], in0=le[:, 1:2], in1=le[:, 0:1], op=ALU.subtract), reads=["le"], writes=["neglam"])
        S.add("dve", lambda e: e.tensor_scalar_add(neglam[:], neglam[:], -LAM_INIT), reads=["neglam"], writes=["neglam"])
        S.add("sp", lambda e: e.dma_start(out=gsub[:], in_=dap(sub_g, 0, [[0, 128], [1, 256]])), writes=["gsub"])
        S.add("dve", lambda e: e.tensor_scalar_mul(gsub[:], gsub[:], 1.0 - LAM_INIT), reads=["gsub"], writes=["gsub"])

        tab = SB(es, "tab", [64, H], F32)
        S.add("dve", lambda e: e.memset(tab[32:64, :], NEG), writes=["tab"])
        S.add("sp", lambda e: e.dma_start(out=tab[0:32, :], in_=rel_tab.ap()), reads=["tab"], writes=["tab"])
        oh = SB(es, "oh", [33, 3 * LU], F32)
        S.add("sp", lambda e: e.dma_start(out=oh[:], in_=oh_all.ap()), writes=["oh"])
        sel = SB(es, "sel", [33, 3 * 128], F32)
        S.add("sp", lambda e: e.dma_start(out=sel[:], in_=sel_all.ap()), writes=["sel"])
        pu = PS(es, "pu", [128, 512])
        usb = SB(es, "usb", [H, 3 * LU], F32)
        for j in range(3 * LU // 512):
            S.add("pe", lambda e, j=j: e.matmul(pu[0:H, :], lhsT=tab[0:33, :], rhs=oh[:, j * 512:(j + 1) * 512], start=True, stop=True),
                  reads=["tab", "oh"], writes=["pu"])
            S.add("dve", lambda e, j=j: e.tensor_copy(usb[:, j * 512:(j + 1) * 512], pu[0:H, :]), reads=["pu"], writes=["usb"])
        S.add("sp", lambda e: e.dma_start(out=dap(UD, 0, [[LU, H], [H * LU, 3], [1, LU]]),
                                          in_=usb[:].rearrange("h (k l) -> h k l", k=3)), reads=["usb"], writes=["UD"])
        S.add("dve", lambda e: e.memset(cls[:], 0.0), writes=["cls"])
        for k in range(3):
            S.add("pe", lambda e, k=k: e.matmul(pu[:, 0:H], lhsT=sel[:, k * 128:(k + 1) * 128], rhs=tab[0:33, :], start=True, stop=True),
                  reads=["tab", "sel"], writes=["pu"])
            S.add("dve", lambda e, k=k: e.tensor_copy(cls[:, k, :], pu[:, 0:H]), reads=["pu", "cls"], writes=["cls"])

        gm = SB(es, "gm", [128, D], F32)
        bcast_load(gm, g_mem, D, "gm")
        memT = SB(es, "memT", [128, KC, NMEM], BF16)
        ptr = [PS(es, f"ptr{i}", [128, D], BF16) for i in range(2)]
        junk = SB(es, "junk0", [128, D], BF16)
        for b in range(2):
            ms = SB(es, f"ms{b}", [128, D], F32)
            hb = SB(es, f"mh{b}", [128, D], BF16)
            ss = SB(es, f"mss{b}", [128, 1], F32); rstd = SB(es, f"mrs{b}", [128, 1], F32)
            S.add("sp", lambda e, b=b, ms=ms: e.dma_start(out=ms[:], in_=mem.ap()[b * 128:(b + 1) * 128, :]), writes=[("ms", b)])
            rms_block(ms[:], ("ms", b), gm, "gm", hb, ("mh", b), junk, ss, rstd, ("m", b))
            transpose_block(hb, ("mh", b), ptr, memT, "memT", b * 128)
        wk = SB(es, "wk", [128, KC, 512], BF16)
        wv = SB(es, "wv", [128, KC, 512], BF16)
        S.add("pq", lambda e: e.dma_start(out=wk[:], in_=w_ck.ap().rearrange("(k p) n -> p k n", p=128)), writes=["wk"])
        S.add("pq", lambda e: e.dma_start(out=wv[:], in_=w_cv.ap().rearrange("(k p) n -> p k n", p=128)), writes=["wv"])
        pk0 = PS(es, "pk0", [128, 512])
        for hd in range(4):
            for kc in range(KC):
                S.add("pe", lambda e, hd=hd, kc=kc: e.matmul(pk0[:, 0:NMEM], lhsT=wk[:, kc, hd * 128:(hd + 1) * 128], rhs=memT[:, kc, :],
                                                           start=(kc == 0), stop=(kc == KC - 1)), reads=["wk", "memT"], writes=["pk0"])
            S.add("dve", lambda e, hd=hd: e.tensor_copy(kct[:, hd, :], pk0[:, 0:NMEM]), reads=["pk0"], writes=["kct"])
        for mb in range(2):
            for kc in range(KC):
                S.add("pe", lambda e, mb=mb, kc=kc: e.matmul(pk0[:, :], lhsT=memT[:, kc, mb * 128:(mb + 1) * 128], rhs=wv[:, kc, :],
                                                           start=(kc == 0), stop=(kc == KC - 1)), reads=["wv", "memT"], writes=["pk0"])
            S.add("dve", lambda e, mb=mb: e.tensor_copy(vc[:, mb, :], pk0[:, :]), reads=["pk0"], writes=["vc"])
        S.barrier()

    SCALE = 128 ** -0.5
    with ExitStack() as es:
        gmx = SB(es, "gmx", [128, D], F32)
        bcast_load(gmx, g_mix, D, "gmx")
        XTs = [SB(es, f"XT{i}", [128, KC, T2], BF16) for i in range(2)]
        xs = [SB(es, f"xs{i}", [128, D], F32) for i in range(2)]
        hbs = [SB(es, f"hb{i}", [128, D], BF16) for i in range(2)]
        junk = SB(es, "junk2", [128, D], BF16)
        sss = [SB(es, f"ss{i}", [128, 1], F32) for i in range(2)]
        rss = [SB(es, f"rs{i}", [128, 1], F32) for i in range(2)]
        wb = [SB(es, f"wb{i}", [128, KC, CW], BF16) for i in range(2)]
        ust = [SB(es, f"ust{i}", [128, T2 // 128, CW], BF16) for i in range(2)]
        vst = [SB(es, f"vst{i}", [128, 2, T2 // 128, VW], BF16) for i in range(2)]
        fst = [SB(es, f"fst{i}", [128, T2], BF16) for i in range(4)]
        ptr = [PS(es, f"ptr2_{i}", [128, D], BF16) for i in range(2)]
        pg = [PS(es, f"pg{i}", [128, 512]) for i in range(4)]
        for i in range(2):
            S.add("pool", lambda e, i=i: e.memset(vst[i][:], 1.0), writes=[("vst", i)])
        w_in_v = w_in.ap().rearrange("(k p) n -> p k n", p=128)
        cnt = {"w": 0, "pg": 0, "u": 0, "v": 0, "f": 0, "x": 0}
        tiles = [("own", t) for t in range(TOWN // T2)] + [("oth", t) for t in range(TOWN // T2)]
        def front_load(ti, b):
            kind_, t_ = tiles[ti]
            src_ = x_own if kind_ == "own" else x_oth
            sl = b % 2
            r0 = t_ * T2 + b * 128
            S.add("sp", lambda e, sl=sl, r0=r0, src_=src_: e.dma_start(out=xs[sl][:], in_=src_.ap()[r0:r0 + 128, :]), writes=[("xs", sl)])

        def front_compute(ti, b):
            sl = b % 2
            rms_block(xs[sl][:], ("xs", sl), gmx, "gmx", hbs[sl], ("hb", sl), junk, sss[sl], rss[sl], ("p2", sl))
            transpose_block(hbs[sl], ("hb", sl), ptr, XTs[ti % 2], ("XT", ti % 2), b * 128)

        NBF = T2 // 128
        front_load(0, 0)
        for b in range(NBF):
            if b + 1 < NBF:
                front_load(0, b + 1)
            front_compute(0, b)
        for ti, (kind, t) in enumerate(tiles):
            XT = XTs[ti % 2]
            xtk = ("XT", ti % 2)
            ktok0 = t * T2 + (0 if kind == "own" else TOWN)
            chunks = list(range(28)) if kind == "own" else list(range(12, 20))
            nl = 0
            ncp = 0
            for jpos, n in enumerate(chunks + [None]):
                if ti + 1 < len(tiles):
                    last = n is None
                    while nl < NBF and (nl <= jpos or last):
                        front_load(ti + 1, nl)
                        nl += 1
                        if last or True:
                            while ncp < nl - 1:
                                front_compute(ti + 1, ncp)
                                ncp += 1
                    if last:
                        while ncp < NBF:
                            front_compute(ti + 1, ncp)
                            ncp += 1
                if n is None:
                    break
                ws = cnt["w"] % 2
                cnt["w"] += 1
                S.add("pq", lambda e, ws=ws, n=n: e.dma_start(out=wb[ws][:], in_=w_in_v[:, :, n * CW:(n + 1) * CW]), writes=[("wb", ws)])
                if n < 8 or 16 <= n < 20:
                    if n < 8:
                        us = cnt["u"] % 2
                        cnt["u"] += 1
                    else:
                        vs = cnt["v"] % 2
                        cnt["v"] += 1
                    for tb in range(T2 // 128):
                        ps_i = cnt["pg"] % 4
                        cnt["pg"] += 1
                        p = pg[ps_i]
                        for kc in range(KC):
                            S.add("pe", lambda e, p=p, ws=ws, kc=kc, tb=tb: e.matmul(p[:, :], lhsT=XT[:, kc, tb * 128:(tb + 1) * 128], rhs=wb[ws][:, kc, :],
                                                                                  start=(kc == 0), stop=(kc == KC - 1)),
                                  reads=[xtk, ("wb", ws)], writes=[("pg", ps_i)])
                        if n < 8:
                            S.add("act", lambda e, p=p, us=us, tb=tb: e.activation(out=ust[us][:, tb, :], in_=p[:, :], func=AF.Gelu_apprx_tanh),
                                  reads=[("pg", ps_i)], writes=[("ust", us)])
                        else:
                            S.add("dve", lambda e, p=p, vs=vs, tb=tb: e.tensor_copy(vst[vs][:, :, tb, 0:256], p[:, :].rearrange("p (h e) -> p h e", h=2)),
                                  reads=[("pg", ps_i)], writes=[("vst", vs)])
                    if n < 8:
                        dst = GU if n < 4 else GVA
                        c0 = (n % 4) * CW
                        r0 = t * T2
                        S.add("sp", lambda e, us=us, dst=dst, c0=c0, r0=r0: e.dma_start(
                            out=dap(dst, r0 * D + c0, [[D, 128], [128 * D, T2 // 128], [1, CW]]), in_=ust[us][:]),
                            reads=[("ust", us)], writes=[("GU", n, t)])
                    else:
                        h0 = (n - 16) * 2
                        kb0 = ktok0 // 128
                        S.add("sp", lambda e, vs=vs, h0=h0, kb0=kb0: e.dma_start(
                            out=dap(VA, h0 * 128 * NKB * VW + kb0 * VW, [[NKB * VW, 128], [128 * NKB * VW, 2], [1, (T2 // 128) * VW]]),
                            in_=vst[vs][:].rearrange("p h k e -> p h (k e)")),
                            reads=[("vst", vs)], writes=[("VA", n, kind, t)])
                else:
                    for fb in range(4):
                        fs = cnt["f"] % 4
                        cnt["f"] += 1
                        for tt in range(T2 // 512 if T2 >= 512 else 1):
                            tw = min(512, T2)
                            ps_i = cnt["pg"] % 4
                            cnt["pg"] += 1
                            p = pg[ps_i]
                            for kc in range(KC):
                                S.add("pe", lambda e, p=p, ws=ws, kc=kc, fb=fb, tt=tt, tw=tw: e.matmul(
                                    p[:, 0:tw], lhsT=wb[ws][:, kc, fb * 128:(fb + 1) * 128], rhs=XT[:, kc, tt * tw:(tt + 1) * tw],
                                    start=(kc == 0), stop=(kc == KC - 1)), reads=[xtk, ("wb", ws)], writes=[("pg", ps_i)])
                            o_ap = lambda fs=fs, tt=tt, tw=tw: fst[fs][:, tt * tw:(tt + 1) * tw]
                            if 8 <= n < 12:
                                S.add("act", lambda e, p=p, o_ap=o_ap, tw=tw: e.mul(out=o_ap(), in_=p[:, 0:tw], mul=SCALE),
                                      reads=[("pg", ps_i)], writes=[("fst", fs)])
                            elif n < 16:
                                S.add("dve", lambda e, p=p, o_ap=o_ap, tw=tw: e.tensor_copy(o_ap(), p[:, 0:tw]),
                                      reads=[("pg", ps_i)], writes=[("fst", fs)])
                            else:
                                S.add("act", lambda e, p=p, o_ap=o_ap, tw=tw: e.activation(out=o_ap(), in_=p[:, 0:tw], func=AF.Sigmoid),
                                      reads=[("pg", ps_i)], writes=[("fst", fs)])
                        if 8 <= n < 12:
                            idx = (n - 8) * 4 + fb
                            dd = dap(QT, idx * 128 * TOWN + t * T2, [[TOWN, 128], [1, T2]])
                        elif n < 16:
                            idx = (n - 12) * 4 + fb
                            dd = dap(KT, idx * 128 * TK + ktok0, [[TK, 128], [1, T2]])
                        elif n < 24:
                            idx = (n - 20) * 4 + fb
                            dd = dap(SGA, idx * 128 * TOWN + t * T2, [[TOWN, 128], [1, T2]])
                        else:
                            idx = (n - 24) * 4 + fb
                            dd = dap(SGB, idx * 128 * TOWN + t * T2, [[TOWN, 128], [1, T2]])
                        S.add("sp", lambda e, fs=fs, dd=dd: e.dma_start(out=dd, in_=fst[fs][:]), reads=[("fst", fs)], writes=[("F", n, fb, kind, t)])
        S.barrier()

    with ExitStack() as es:
        wpa = SB(es, "wpa", [128, KC, D], BF16)
        for j in range(4):
            S.add("pq", lambda e, j=j: e.dma_start(out=wpa[:, :, j * 512:(j + 1) * 512],
                                                    in_=w_pa.ap().rearrange("(k p) n -> p k n", p=128)[:, :, j * 512:(j + 1) * 512]), writes=[("wpa", j)])
        wpa_keys = [("wpa", j) for j in range(4)]
        lng = SB(es, "lng", [128, D], F32); lnb = SB(es, "lnb", [128, D], F32)
        bcast_load(lng, ln_g, D, "lng"); bcast_load(lnb, ln_b, D, "lnb")
        wsb = SB(es, "wsb", [128, 8, 128], BF16)
        S.add("pq", lambda e: e.dma_start(out=wsb[:], in_=w_sp.ap().rearrange("g i j -> i g j")), writes=["wsb"])
        wsT = SB(es, "wsT", [128, 8, 128], BF16)
        ptr = [PS(es, "ptr3_0", [128, D], BF16)]
        for g in range(8):
            S.add("pe", lambda e, g=g: e.transpose(ptr[0][:, g * 128:(g + 1) * 128], wsb[:, g, :], ident[:]), reads=["wsb", "ident"], writes=[("ptr", 0)])
        S.add("dve", lambda e: e.tensor_copy(wsT[:], ptr[0][:, 0:1024].rearrange("p (g i) -> p g i", g=8)), reads=[("ptr", 0)], writes=["wsT"])
        bsT = SB(es, "bsT", [128, 8], F32)
        S.add("sp", lambda e: e.dma_start(out=bsT[:], in_=dap(b_sp, 0, [[1, 128], [128, 8]]), allow_slow_non_contiguous=True), writes=["bsT"])
        OATs = [SB(es, f"OAT{i}", [128, KC, T3], BF16) for i in range(2)]
        gub = [SB(es, f"gub{i}", [128, D], BF16) for i in range(4)]
        gvb = [SB(es, f"gvb{i}", [128, D], BF16) for i in range(4)]
        t1s = [SB(es, f"t1_{i}", [128, D], F32) for i in range(2)]
        vns = [SB(es, f"vn_{i}", [128, D], BF16) for i in range(2)]
        oab = [SB(es, f"oab{i}", [128, D], BF16) for i in range(2)]
        junk = SB(es, "junk3", [128, D], BF16)
        sts = [SB(es, f"st3_{i}", [128, 8], F32) for i in range(2)]
        pm = PS(es, "pm", [128, D])
        pgm = [PS(es, f"pg3_{i}", [128, 512]) for i in range(2)]
        sgas = [SB(es, f"sga{i}", [128, KC, T3], BF16) for i in range(2)]
        a1ss = [SB(es, f"a1s{i}", [128, KC, T3], BF16) for i in range(2)]
        NB3 = TOWN // 128
        BPT = T3 // 128
        NT3 = TOWN // T3
        pcnt = [0]

        def load_blk(i):
            s3 = i % 4
            r0 = i * 128
            S.add("sp", lambda e, s3=s3, r0=r0: e.dma_start(out=gub[s3][:], in_=GU.ap()[r0:r0 + 128, :]), writes=[("gub", s3)])
            S.add("sp", lambda e, s3=s3, r0=r0: e.dma_start(out=gvb[s3][:], in_=GVA.ap()[r0:r0 + 128, :]), writes=[("gvb", s3)])

        def load_sga(t):
            S.add("sp", lambda e, t=t: e.dma_start(out=sgas[t % 2][:], in_=dap(SGA, t * T3, [[TOWN, 128], [128 * TOWN, KC], [1, T3]])), writes=[("sga", t % 2)])

        def stage_a_head(i):
            s4 = i % 4
            sl = i % 2
            st = sts[sl]
            S.add("act", lambda e: e.activation(out=junk[:], in_=gvb[s4][:], func=AF.Identity, accum_out=st[:, 0:1]),
                  reads=[("gvb", s4)], writes=[("st", sl, 0)])
            S.add("act", lambda e: e.activation(out=junk[:], in_=gvb[s4][:], func=AF.Square, accum_out=st[:, 1:2]),
                  reads=[("gvb", s4)], writes=[("st", sl, 1)])
            S.add("dve", lambda e: e.tensor_scalar_mul(st[:, 2:4], st[:, 0:2], 1.0 / D), reads=[("st", sl, 0), ("st", sl, 1)], writes=[("st", sl, 2)])
            S.add("dve", lambda e: e.tensor_tensor(out=st[:, 4:5], in0=st[:, 2:3], in1=st[:, 2:3], op=ALU.mult), reads=[("st", sl, 2)], writes=[("st", sl, 4)])
            S.add("dve", lambda e: e.tensor_tensor(out=st[:, 5:6], in0=st[:, 3:4], in1=st[:, 4:5], op=ALU.subtract), reads=[("st", sl, 2), ("st", sl, 4)], writes=[("st", sl, 5)])
            S.add("act", lambda e: e.activation(out=st[:, 6:7], in_=st[:, 5:6], func=AF.Ln, bias=epsc[:, 0:1], scale=1.0),
                  reads=[("st", sl, 5), "epsc"], writes=[("st", sl, 6)])
            S.add("act", lambda e: e.activation(out=st[:, 7:8], in_=st[:, 6:7], func=AF.Exp, scale=-0.5), reads=[("st", sl, 6)], writes=[("st", sl, 7)])
            S.add("dve", lambda e: e.tensor_scalar(out=st[:, 4:5], in0=st[:, 2:3], scalar1=st[:, 7:8], scalar2=-1.0, op0=ALU.mult, op1=ALU.mult),
                  reads=[("st", sl, 2), ("st", sl, 7), ("st", sl, 5)], writes=[("st", sl, 4)])

        def stage_a_tail1(i):
            s4 = i % 4
            sl = i % 2
            t1 = t1s[sl]; st = sts[sl]
            S.add("act", lambda e: e.activation(out=t1[:], in_=gvb[s4][:], func=AF.Identity, bias=st[:, 4:5], scale=st[:, 7:8]),
                  reads=[("gvb", s4), ("st", sl, 4), ("st", sl, 7)], writes=[("t1", sl)])
            S.add("pool", lambda e: e.tensor_tensor(out=t1[:], in0=t1[:], in1=lng[:], op=ALU.mult), reads=[("t1", sl), "lng"], writes=[("t1", sl)])

        def stage_a_tail2(i):
            sl = i % 2
            t1 = t1s[sl]; vn = vns[sl]
            S.add("dve", lambda e: e.tensor_tensor(out=vn[:], in0=t1[:], in1=lnb[:], op=ALU.add), reads=[("t1", sl), "lnb"], writes=[("vn", sl)])

        def stage_b1(i):
            s3 = i % 4
            sl = i % 2
            vn = vns[sl]
            for g in range(8):
                S.add("pe", lambda e, g=g: e.matmul(pm[:, g * 256:(g + 1) * 256], lhsT=wsT[:, g, :], rhs=vn[:, g * 256:(g + 1) * 256], start=True, stop=True),
                      reads=["wsT", ("vn", sl)], writes=[("pm", g)])
            for g in range(8):
                S.add("dve", lambda e, g=g: e.scalar_tensor_tensor(out=oab[sl][:, g * 256:(g + 1) * 256], in0=pm[:, g * 256:(g + 1) * 256],
                                                                  scalar=bsT[:, g:g + 1], in1=gub[s3][:, g * 256:(g + 1) * 256], op0=ALU.add, op1=ALU.mult),
                      reads=[("pm", g), "bsT", ("gub", s3)], writes=[("oab", sl)])

        def stage_b2(i):
            sl = i % 2
            t = i // BPT
            transpose_block(oab[sl], ("oab", sl), ptr, OATs[t % 2], ("OAT", t % 2), (i % BPT) * 128)

        def gemm(t, fbs, last):
            OAT = OATs[t % 2]; sga = sgas[t % 2]; a1s = a1ss[t % 2]
            for fb in fbs:
                pi = pcnt[0] % 2
                pcnt[0] += 1
                for kc in range(KC):
                    S.add("pe", lambda e, pi=pi, kc=kc, fb=fb: e.matmul(pgm[pi][:, 0:T3], lhsT=wpa[:, kc, fb * 128:(fb + 1) * 128], rhs=OAT[:, kc, :],
                                                                     start=(kc == 0), stop=(kc == KC - 1)), reads=wpa_keys + [("OAT", t % 2)], writes=[("pg3", pi)])
                S.add("dve", lambda e, pi=pi, fb=fb: e.tensor_tensor(out=a1s[:, fb, :], in0=pgm[pi][:, 0:T3], in1=sga[:, fb, :], op=ALU.mult),
                      reads=[("pg3", pi), ("sga", t % 2)], writes=[("a1s", t % 2)])
            if last:
                S.add("sp", lambda e, t=t: e.dma_start(out=dap(A1T, t * T3, [[TOWN, 128], [128 * TOWN, KC], [1, T3]]), in_=a1s[:]),
                      reads=[("a1s", t % 2)], writes=[("A1T", t)])
                if t + 2 < NT3:
                    load_sga(t + 2)

        for i in range(min(4, NB3)):
            load_blk(i)
        for t in range(min(2, NT3)):
            load_sga(t)
        stage_a_head(0)
        stage_a_tail1(0)
        stage_a_tail2(0)
        if NB3 > 1:
            stage_a_head(1)
        gq = []
        NPIECE = BPT
        for i in range(NB3):
            if i + 2 < NB3:
                stage_a_head(i + 2)
            if i + 1 < NB3:
                stage_a_tail1(i + 1)
            stage_b1(i)
            if gq:
                gemm(*gq.pop(0))
            stage_b2(i)
            if i + 1 < NB3:
                stage_a_tail2(i + 1)
            if i + 4 < NB3:
                load_blk(i + 4)
            if (i + 1) % BPT == 0:
                t = i // BPT
                per = KC // NPIECE
                for pz in range(NPIECE):
                    gq.append((t, list(range(pz * per, (pz + 1) * per if pz < NPIECE - 1 else KC)), pz == NPIECE - 1))
        while gq:
            gemm(*gq.pop(0))
        S.barrier()

    with ExitStack() as es:
        kt = [SB(es, f"kt{m}", [128, TK], BF16) for m in range(2)]
        va = SB(es, "va", [128, NKB, VW], BF16)
        hbt = SB(es, "hbt", [128, 12, QC], BF16)
        qt = [SB(es, f"qt{i}", [128, QC], BF16) for i in range(4)]
        pt = [SB(es, f"pt{i}", [128, 2, QC], BF16) for i in range(3)]
        o1s = SB(es, "o1s", [128, QS, 257], F32)
        o2s = SB(es, "o2s", [128, QS, 257], F32)
        junkf = SB(es, "junkf4", [128, 256], F32)
        osb = SB(es, "osb", [128, QS, 256], F32)
        obst = [SB(es, f"obst{i}", [128, QS, 256], BF16) for i in range(2)]
        sm = SB(es, "sm4", [128, 4 * QS], F32)
        junk = SB(es, "junk4", [128, 256], BF16)
        spair = [PS(es, f"sp{i}", [128, 2, 512]) for i in range(2)]
        acc = [PS(es, f"acc{i}", [128, 512]) for i in range(QS)]
        deltas = [-256 + 128 * i for i in range(QS + 4)]
        ND = len(deltas)
        qcount = 0
        pcount = 0
        ocount = 0
        pending = []
        for h in range(H):
            NPC = 4 if NKB % 4 == 0 else 1
            KPB = NKB // NPC
            for m in range(2):
                for pc4 in range(NPC):
                    S.add("sp", lambda e, h=h, m=m, pc4=pc4: e.dma_start(out=kt[m][:, pc4 * KPB * 128:(pc4 + 1) * KPB * 128],
                                                                         in_=dap(KT, (2 * h + m) * 128 * TK + pc4 * KPB * 128, [[TK, 128], [1, KPB * 128]])),
                          writes=[("kt", m, pc4)])
                if m == 0:
                    for pc4 in range(NPC):
                        S.add("sp", lambda e, h=h, pc4=pc4: e.dma_start(out=va[:, pc4 * KPB:(pc4 + 1) * KPB, :],
                                                                       in_=dap(VA, h * 128 * NKB * VW + pc4 * KPB * VW, [[NKB * VW, 128], [1, KPB * VW]])),
                              writes=[("va", pc4)])
            for i, dl in enumerate(deltas):
                off = U0 - dl - 127
                S.add("pq", lambda e, i=i, off=off, h=h: e.dma_start(out=hbt[:, i, :], in_=dap(UD, (0 * H + h) * LU + off, [[1, 128], [1, QC]])),
                      reads=["UD"], writes=[("hbt", i)])
            for i, dl in enumerate((QC, QC + 128)):
                off = U0 - dl - 127
                S.add("pq", lambda e, i=i, off=off, h=h: e.dma_start(out=hbt[:, ND + i, :], in_=dap(UD, (1 * H + h) * LU + off, [[1, 128], [1, QC]])),
                      reads=["UD"], writes=[("hbt", ND + i)])
            for i, dl in enumerate((-256, -128)):
                off = U0 - dl - 127
                S.add("pq", lambda e, i=i, off=off, h=h: e.dma_start(out=hbt[:, ND + 2 + i, :], in_=dap(UD, (2 * H + h) * LU + off, [[1, 128], [1, QC]])),
                      reads=["UD"], writes=[("hbt", ND + 2 + i)])

            if h == 0:
                for srcw, dstw, rows in ((w_pb, WPB_B, D), (w_out, WOUT_B, D), (w_cq, WCQ_B, D), (w_co, WCO_B, 512), (w_fi, WFI_B, D), (w_fo, WFO_B, DFF)):
                    npc = 4
                    rr = rows // npc
                    for pc4 in range(npc):
                        S.add("pq", lambda e, srcw=srcw, dstw=dstw, pc4=pc4, rr=rr: e.dma_start(out=dstw.ap()[pc4 * rr:(pc4 + 1) * rr, :], in_=srcw.ap()[pc4 * rr:(pc4 + 1) * rr, :]),
                              writes=[("wconv", dstw.name, pc4)])
            for c in range(NQC):
                for m in range(2):
                    u_ = (h * NQC + c) * 2 + m
                    qs_ = u_ % 4
                    for ua in ((u_, u_ + 1, u_ + 2) if u_ == 0 else (u_ + 2,)):
                        if ua < H * NQC * 2:
                            h2, c2, m2 = ua // (2 * NQC), (ua // 2) % NQC, ua % 2
                            S.add("sp", lambda e, ua=ua, h2=h2, m2=m2, c2=c2: e.dma_start(out=qt[ua % 4][:], in_=dap(QT, (2 * h2 + m2) * 128 * TOWN + c2 * QC, [[TOWN, 128], [1, QC]])),
                                  writes=[("qt", ua % 4)])

                    def pair_info(kp):
                        tiles = []
                        for j in range(2):
                            kb = 2 * kp + j
                            if kb < NOB:
                                dlt = kb * 128 - c * QC
                                if -256 <= dlt <= QC + 128:
                                    tiles.append(deltas.index(dlt))
                                else:
                                    tiles.append("LO" if dlt < 0 else "HI")
                            else:
                                ko = kb - NOB
                                if c == NQC - 1 and ko < 2:
                                    tiles.append(ND + ko)
                                elif c == 0 and ko >= NOB - 2:
                                    tiles.append(ND + 2 + (ko - (NOB - 2)))
                                else:
                                    tiles.append("OTH")
                        return tiles

                    def emit_qk(kp):
                        sl = kp % 2
                        tiles = pair_info(kp)
                        use_mm = any(isinstance(x, int) for x in tiles)
                        for j in range(2):
                            kb = 2 * kp + j
                            S.add("pe", lambda e, sl=sl, j=j, kb=kb, qs_=qs_, use_mm=use_mm: e.matmul(
                                spair[sl][:, j, 0:QC], lhsT=kt[m][:, kb * 128:(kb + 1) * 128], rhs=qt[qs_][:, :], start=True, stop=not use_mm),
                                reads=[("kt", m, kb // KPB), ("qt", qs_)], writes=[("spair", sl)])
                            if use_mm:
                                ti = tiles[j]
                                assert isinstance(ti, int), "mixed near/far pair not supported"
                                S.add("pe", lambda e, sl=sl, j=j, ti=ti: e.matmul(spair[sl][:, j, 0:QC], lhsT=jrev[:], rhs=hbt[:, ti, :], start=False, stop=True),
                                      reads=["jrev", ("hbt", ti)], writes=[("spair", sl)])
                        if use_mm:
                            ci = 3
                        else:
                            assert tiles[0] == tiles[1]
                            ci = {"LO": 0, "HI": 1, "OTH": 2}[tiles[0]]
                        return ci

                    def emit_exp(kp, ci, ps_):
                        sl = kp % 2
                        S.add("act", lambda e, sl=sl, ps_=ps_, ci=ci: e.activation(out=pt[ps_][:, :, :], in_=spair[sl][:, :, 0:QC], func=AF.Exp,
                                                                              bias=cls[:, ci, h:h + 1], scale=1.0),
                              reads=[("spair", sl), "cls"], writes=[("pt", ps_)])

                    def emit_pv(kp, ps_):
                        for j in range(2):
                            kb = 2 * kp + j
                            for q in range(QS):
                                S.add("pe", lambda e, ps_=ps_, j=j, kb=kb, q=q: e.matmul(acc[q][:, 0:257], lhsT=pt[ps_][:, j, q * 128:(q + 1) * 128], rhs=va[:, kb, 0:257],
                                                                                    start=(kb == 0), stop=(kb == NKB - 1)),
                                      reads=[("pt", ps_), ("va", kb // KPB)], writes=[("acc", q)])

                    NP = NKB // 2
                    pend = None
                    for kp in range(NP):
                        if pending and kp == min(2, NP - 1) and pending[0][0] is not None:
                            pending[0][0]()
                            pending[0][0] = None
                        if pending and kp == min(14, NP - 1) and pending[0][0] is None:
                            pending.pop(0)[1]()
                        ci = emit_qk(kp)
                        ps_ = pcount % 3
                        pcount += 1
                        emit_exp(kp, ci, ps_)
                        if pend is not None:
                            emit_pv(*pend)
                        pend = (kp, ps_)
                    emit_pv(*pend)

                    if m == 0:
                        for q in range(QS):
                            S.add("dve", lambda e, q=q: e.tensor_copy(o1s[:, q, :], acc[q][:, 0:257]), reads=[("acc", q)], writes=[("o1s", q)])
                    else:
                        ob_ = ocount % 2
                        ocount += 1
                        for q in range(QS):
                            S.add("dve", lambda e, q=q: e.tensor_copy(o2s[:, q, :], acc[q][:, 0:257]), reads=[("acc", q)], writes=[("o2s", q)])

                        def part1():
                            for q in range(QS):
                                S.add("dve", lambda e, q=q: e.reciprocal(sm[:, q:q + 1], o1s[:, q, 256:257]), reads=[("o1s", q)], writes=[("sm", q)])
                                S.add("dve", lambda e, q=q: e.reciprocal(sm[:, QS + q:QS + q + 1], o2s[:, q, 256:257]), reads=[("o2s", q)], writes=[("sm", QS + q)])
                                S.add("dve", lambda e, q=q: e.tensor_tensor(out=sm[:, QS + q:QS + q + 1], in0=sm[:, QS + q:QS + q + 1], in1=neglam[:], op=ALU.mult),
                                      reads=[("sm", QS + q), "neglam"], writes=[("sm", QS + q)])
                                S.add("dve", lambda e, q=q: e.tensor_scalar_mul(osb[:, q, :], o1s[:, q, 0:256], sm[:, q:q + 1]),
                                      reads=[("o1s", q), ("sm", q)], writes=[("osb", q)])
                                S.add("dve", lambda e, q=q: e.scalar_tensor_tensor(out=osb[:, q, :], in0=o2s[:, q, 0:256], scalar=sm[:, QS + q:QS + q + 1], in1=osb[:, q, :],
                                                                                  op0=ALU.mult, op1=ALU.add),
                                      reads=[("o2s", q), ("sm", QS + q), ("osb", q)], writes=[("osb", q)])
                                S.add("dve", lambda e, q=q: e.scalar_tensor_tensor(out=junkf[:], in0=osb[:, q, :], scalar=1.0, in1=osb[:, q, :], op0=ALU.mult, op1=ALU.mult,
                                                                                  accum_out=sm[:, 2 * QS + q:2 * QS + q + 1]),
                                      reads=[("osb", q)], writes=["junk4", ("sm", 2 * QS + q)])

                        def part2(ob_=ob_, c=c, h=h):
                            S.add("act", lambda e: e.activation(out=sm[:, 3 * QS:4 * QS], in_=sm[:, 2 * QS:3 * QS], func=AF.Ln, bias=epsc[:, 1:2], scale=1.0 / 256),
                                  reads=[("sm", 2 * QS + q) for q in range(QS)] + ["epsc"], writes=[("sm", 3 * QS + q) for q in range(QS)])
                            S.add("act", lambda e: e.activation(out=sm[:, 3 * QS:4 * QS], in_=sm[:, 3 * QS:4 * QS], func=AF.Exp, scale=-0.5),
                                  reads=[("sm", 3 * QS + q) for q in range(QS)], writes=[("sm", 3 * QS + q) for q in range(QS)])
                            for q in range(QS):
                                S.add("dve", lambda e, q=q: e.scalar_tensor_tensor(out=obst[ob_][:, q, :], in0=osb[:, q, :], scalar=sm[:, 3 * QS + q:3 * QS + q + 1],
                                                                                  in1=gsub[:], op0=ALU.mult, op1=ALU.mult),
                                      reads=[("osb", q), ("sm", 3 * QS + q), "gsub"], writes=[("obst", ob_)])
                            S.add("sp", lambda e: e.dma_start(out=dap(OB, c * QC * D + h * 256, [[D, 128], [128 * D, QS], [1, 256]]), in_=obst[ob_][:]),
                                  reads=[("obst", ob_)], writes=[("OB", c, h)])
                        pending.append([part1, part2])
        while pending:
            p1, p2 = pending.pop(0)
            if p1 is not None:
                p1()
            p2()
        S.barrier()

    T5 = min(512, TOWN)
    T5A = min(256, TOWN)
    with ExitStack() as es:
        wpb = SB(es, "wpb", [128, KC, D], BF16)
        for j in range(4):
            S.add("pq", lambda e, j=j: e.dma_start(out=wpb[:, :, j * 512:(j + 1) * 512],
                                                    in_=WPB_B.ap().rearrange("(k p) n -> p k n", p=128)[:, :, j * 512:(j + 1) * 512]), writes=[("wpb", j)])
        wkeys = [("wpb", j) for j in range(4)]
        BPT5 = T5A // 128
        obl = [SB(es, f"obl{i}", [128, D], BF16) for i in range(2 * BPT5)]
        OBTs = [SB(es, f"OBT{i}", [128, KC, T5A], BF16) for i in range(2)]
        sgbs = [SB(es, f"sgb{i}", [128, KC, T5A], BF16) for i in range(2)]
        a1ls = [SB(es, f"a1l{i}", [128, KC, T5A], BF16) for i in range(2)]
        msts = [SB(es, f"mst{i}", [128, KC, T5A], BF16) for i in range(2)]
        tmp = [SB(es, f"tmp5{i}", [128, T5A], F32) for i in range(2)]
        ptr = [PS(es, f"ptr5_{i}", [128, D], BF16) for i in range(2)]
        pg5 = [PS(es, f"pg5_{i}", [128, 512]) for i in range(4)]
        NT5 = TOWN // T5A
        pcnt = [0]

        def loads5(t):
            ts_ = t % 2
            S.add("sp", lambda e, t=t: e.dma_start(out=sgbs[ts_][:], in_=dap(SGB, t * T5A, [[TOWN, 128], [128 * TOWN, KC], [1, T5A]])), writes=[("sgb", ts_)])
            S.add("sp", lambda e, t=t: e.dma_start(out=a1ls[ts_][:], in_=dap(A1T, t * T5A, [[TOWN, 128], [128 * TOWN, KC], [1, T5A]])), writes=[("a1l", ts_)])
            for b_ in range(BPT5):
                r0 = t * T5A + b_ * 128
                os_ = ts_ * BPT5 + b_
                S.add("sp", lambda e, os_=os_, r0=r0: e.dma_start(out=obl[os_][:], in_=OB.ap()[r0:r0 + 128, :]), writes=[("obl", os_)])

        def trans5(t):
            ts_ = t % 2
            for b_ in range(BPT5):
                os_ = ts_ * BPT5 + b_
                transpose_block(obl[os_], ("obl", os_), ptr, OBTs[ts_], ("OBT", ts_), b_ * 128)

        def gemm5(t):
            ts_ = t % 2
            OBT = OBTs[ts_]; sgb = sgbs[ts_]; a1l = a1ls[ts_]; mst = msts[ts_]
            for fb in range(KC):
                pi = pcnt[0] % 4
                pcnt[0] += 1
                for kc in range(KC):
                    S.add("pe", lambda e, pi=pi, kc=kc, fb=fb: e.matmul(pg5[pi][:, 0:T5A], lhsT=wpb[:, kc, fb * 128:(fb + 1) * 128], rhs=OBT[:, kc, :],
                                                                     start=(kc == 0), stop=(kc == KC - 1)), reads=wkeys + [("OBT", ts_)], writes=[("pg5", pi)])
                S.add("dve", lambda e, pi=pi, fb=fb: e.tensor_tensor(out=tmp[pi % 2][:], in0=pg5[pi][:, 0:T5A], in1=sgb[:, fb, :], op=ALU.mult),
                      reads=[("pg5", pi), ("sgb", ts_)], writes=[("tmp5", pi % 2)])
                S.add("pool", lambda e, pi=pi, fb=fb: e.tensor_tensor(out=mst[:, fb, :], in0=tmp[pi % 2][:], in1=a1l[:, fb, :], op=ALU.add),
                      reads=[("tmp5", pi % 2), ("a1l", ts_)], writes=[("mst", ts_)])
            S.add("sp", lambda e, t=t: e.dma_start(out=dap(MT, t * T5A, [[TOWN, 128], [128 * TOWN, KC], [1, T5A]]), in_=mst[:]),
                  reads=[("mst", ts_)], writes=[("MT", t)])

        for t in range(min(2, NT5)):
            loads5(t)
        trans5(0)
        for t in range(NT5):
            if t + 1 < NT5:
                trans5(t + 1)
            gemm5(t)
            if t + 2 < NT5:
                loads5(t + 2)
        S.barrier()

    with ExitStack() as es:
        wo = SB(es, "wo", [128, KC, D], BF16)
        for j in range(4):
            S.add("pq", lambda e, j=j: e.dma_start(out=wo[:, :, j * 512:(j + 1) * 512],
                                                    in_=WOUT_B.ap().rearrange("(k p) n -> p k n", p=128)[:, :, j * 512:(j + 1) * 512]), writes=[("wo", j)])
        wkeys = [("wo", j) for j in range(4)]
        mtl = [SB(es, f"mtl{i}", [128, KC, T5], BF16) for i in range(2)]
        xl = [SB(es, f"xl{i}", [128, D], F32) for i in range(3)]
        pg5 = [PS(es, f"pg5b_{i}", [128, 512]) for i in range(4)]
        NT5B = TOWN // T5
        BPB = T5 // 128
        NG = TOWN // 128
        pcnt = [0]

        def load_mt(t):
            S.add("sp", lambda e, t=t: e.dma_start(out=mtl[t % 2][:], in_=dap(MT, t * T5, [[TOWN, 128], [128 * TOWN, KC], [1, T5]])), writes=[("mtl", t % 2)])

        def load_x(g):
            S.add("sp", lambda e, g=g: e.dma_start(out=xl[g % 3][:], in_=x_own.ap()[g * 128:(g + 1) * 128, :]), writes=[("xl", g % 3)])

        def comp5b(g):
            t = g // BPB
            b_ = g % BPB
            ms_ = t % 2
            sl = g % 3
            for n in range(4):
                pi = pcnt[0] % 4
                pcnt[0] += 1
                for kc in range(KC):
                    S.add("pe", lambda e, pi=pi, kc=kc, n=n: e.matmul(pg5[pi][:, :], lhsT=mtl[ms_][:, kc, b_ * 128:(b_ + 1) * 128], rhs=wo[:, kc, n * 512:(n + 1) * 512],
                                                                   start=(kc == 0), stop=(kc == KC - 1)), reads=wkeys + [("mtl", ms_)], writes=[("pg5b", pi)])
                S.add("dve", lambda e, pi=pi, n=n: e.tensor_tensor(out=xl[sl][:, n * 512:(n + 1) * 512], in0=pg5[pi][:, :], in1=xl[sl][:, n * 512:(n + 1) * 512], op=ALU.add),
                      reads=[("pg5b", pi), ("xl", sl)], writes=[("xl", sl)])
            S.add("sp", lambda e, g=g: e.dma_start(out=X1.ap()[g * 128:(g + 1) * 128, :], in_=xl[sl][:]), reads=[("xl", sl)], writes=[("X1", g)])

        for t in range(min(2, NT5B)):
            load_mt(t)
        for g in range(min(2, NG)):
            load_x(g)
        for g in range(NG):
            if g + 2 < NG:
                load_x(g + 2)
            comp5b(g)
            if (g + 1) % BPB == 0 and g // BPB + 2 < NT5B:
                load_mt(g // BPB + 2)
        S.barrier()

    SC_C = 128 ** -0.5
    T6 = min(512, TOWN)
    with ExitStack() as es:
        gcr = SB(es, "gcr", [128, D], F32)
        bcast_load(gcr, g_cross, D, "gcr")
        wq = SB(es, "wq", [128, KC, 512], BF16)
        S.add("pq", lambda e: e.dma_start(out=wq[:], in_=WCQ_B.ap().rearrange("(k p) n -> p k n", p=128)), writes=["wq"])
        wco = SB(es, "wco", [128, 4, D], BF16)
        S.add("pq", lambda e: e.dma_start(out=wco[:], in_=WCO_B.ap().rearrange("(k p) n -> p k n", p=128)), writes=["wco"])
        BP6 = T6 // 128
        x1l = [SB(es, f"x1l{i}", [128, D], F32) for i in range(2 * BP6)]
        hb6 = [SB(es, f"hb6{i}", [128, D], BF16) for i in range(BP6)]
        junk = SB(es, "junk6", [128, D], BF16)
        ss6 = [SB(es, f"ss6{i}", [128, 2], F32) for i in range(BP6)]
        H2Ts = [SB(es, f"H2T{i}", [128, KC, T6], BF16) for i in range(2)]
        qc = SB(es, "qc", [128, 4, T6], BF16)
        pc = [SB(es, f"pc{i}", [128, 2, T6], BF16) for i in range(2)]
        rl = [SB(es, f"rl{i}", [128, T6], F32) for i in range(2)]
        oc = SB(es, "oc", [128, 4, T6], BF16)
        ptr = [PS(es, "ptr6_0", [128, D], BF16)]
        pq6 = PS(es, "pq6", [128, 512])
        ps6 = PS(es, "ps6", [128, 2, 512])
        po6 = PS(es, "po6", [128, 512])
        pl6 = PS(es, "pl6", [128, 512])
        NT6 = TOWN // T6
        obanks = [(pq6[:, :], "pq6"), (po6[:, :], "po6"), (pl6[:, :], "pl6"), (ps6[:, 0, :], ("ps6", 0)), (ps6[:, 1, :], ("ps6", 1))]
        ocnt = [0]

        def loads6(t):
            for b_ in range(BP6):
                xs_ = (t % 2) * BP6 + b_
                r0 = t * T6 + b_ * 128
                S.add("sp", lambda e, xs_=xs_, r0=r0: e.dma_start(out=x1l[xs_][:], in_=X1.ap()[r0:r0 + 128, :]), writes=[("x1l", xs_)])

        def rms6(t):
            for b_ in range(BP6):
                xs_ = (t % 2) * BP6 + b_
                S.add("act", lambda e, xs_=xs_, b_=b_: e.activation(out=junk[:], in_=x1l[xs_][:], func=AF.Square, accum_out=ss6[b_][:, 0:1]),
                      reads=[("x1l", xs_)], writes=[("ss6", b_)])
                S.add("act", lambda e, b_=b_: e.activation(out=ss6[b_][:, 1:2], in_=ss6[b_][:, 0:1], func=AF.Ln, bias=epsc[:, 0:1], scale=1.0 / D),
                      reads=[("ss6", b_), "epsc"], writes=[("ss6", b_)])
                S.add("act", lambda e, b_=b_: e.activation(out=ss6[b_][:, 1:2], in_=ss6[b_][:, 1:2], func=AF.Exp, scale=-0.5),
                      reads=[("ss6", b_)], writes=[("ss6", b_)])
                S.add("dve", lambda e, xs_=xs_, b_=b_: e.scalar_tensor_tensor(out=hb6[b_][:], in0=x1l[xs_][:], scalar=ss6[b_][:, 1:2], in1=gcr[:], op0=ALU.mult, op1=ALU.mult),
                      reads=[("x1l", xs_), ("ss6", b_), "gcr"], writes=[("hb6", b_)])

        def trans6(t, b_):
            transpose_block(hb6[b_], ("hb6", b_), ptr, H2Ts[t % 2], ("H2T", t % 2), b_ * 128)

        def back6(t, nxt):
            H2T = H2Ts[t % 2]
            hk = ("H2T", t % 2)
            if nxt is not None:
                rms6(nxt)
            for hd in range(4):
                for kc in range(KC):
                    S.add("pe", lambda e, hd=hd, kc=kc: e.matmul(pq6[:, 0:T6], lhsT=wq[:, kc, hd * 128:(hd + 1) * 128], rhs=H2T[:, kc, :],
                                                               start=(kc == 0), stop=(kc == KC - 1)), reads=["wq", hk], writes=["pq6"])
                S.add("act", lambda e, hd=hd: e.mul(out=qc[:, hd, :], in_=pq6[:, 0:T6], mul=SC_C), reads=["pq6"], writes=[("qc", hd)])

            def s_mm(j):
                hd, mb = j // 2, j % 2
                S.add("pe", lambda e, hd=hd, mb=mb: e.matmul(ps6[:, mb, 0:T6], lhsT=kct[:, hd, mb * 128:(mb + 1) * 128], rhs=qc[:, hd, :], start=True, stop=True),
                      reads=["kct", ("qc", hd)], writes=[("ps6", mb)])
                S.add("act", lambda e, hd=hd, mb=mb: e.activation(out=pc[hd % 2][:, mb, :], in_=ps6[:, mb, 0:T6], func=AF.Exp), reads=[("ps6", mb)], writes=[("pc", hd % 2, mb)])

            def pv_mm(j):
                hd, mb = j // 2, j % 2
                pcs = hd % 2
                S.add("pe", lambda e, hd=hd, mb=mb, pcs=pcs: e.matmul(po6[:, 0:T6], lhsT=vc[:, mb, hd * 128:(hd + 1) * 128], rhs=pc[pcs][:, mb, :], start=(mb == 0), stop=(mb == 1)),
                      reads=["vc", ("pc", pcs, mb)], writes=["po6"])
                S.add("pe", lambda e, mb=mb, pcs=pcs: e.matmul(pl6[:, 0:T6], lhsT=onesb[:], rhs=pc[pcs][:, mb, :], start=(mb == 0), stop=(mb == 1)),
                      reads=["onesb", ("pc", pcs, mb)], writes=["pl6"])
                if mb == 1:
                    S.add("act", lambda e, pcs=pcs: e.activation(out=rl[pcs][:], in_=pl6[:, 0:T6], func=AF.Ln), reads=["pl6"], writes=[("rl", pcs)])
                    S.add("act", lambda e, pcs=pcs: e.activation(out=rl[pcs][:], in_=rl[pcs][:], func=AF.Exp, scale=-1.0), reads=[("rl", pcs)], writes=[("rl", pcs)])
                    S.add("dve", lambda e, hd=hd, pcs=pcs: e.tensor_tensor(out=oc[:, hd, :], in0=po6[:, 0:T6], in1=rl[pcs][:], op=ALU.mult), reads=["po6", ("rl", pcs)], writes=[("oc", hd)])

            s_mm(0)
            for j in range(8):
                if j + 1 < 8:
                    s_mm(j + 1)
                pv_mm(j)
            for b_ in range(BP6):
                xs_ = (t % 2) * BP6 + b_
                r0 = t * T6 + b_ * 128
                for n in range(4):
                    bank, bkey = obanks[ocnt[0] % len(obanks)]
                    ocnt[0] += 1
                    for hd in range(4):
                        S.add("pe", lambda e, hd=hd, n=n, b_=b_, bank=bank: e.matmul(bank, lhsT=oc[:, hd, b_ * 128:(b_ + 1) * 128], rhs=wco[:, hd, n * 512:(n + 1) * 512],
                                                                                   start=(hd == 0), stop=(hd == 3)), reads=["wco"] + [("oc", i) for i in range(4)], writes=[bkey])
                    S.add("dve", lambda e, n=n, xs_=xs_, bank=bank: e.tensor_tensor(out=x1l[xs_][:, n * 512:(n + 1) * 512], in0=bank, in1=x1l[xs_][:, n * 512:(n + 1) * 512], op=ALU.add),
                          reads=[bkey, ("x1l", xs_)], writes=[("x1l", xs_)])
                S.add("sp", lambda e, xs_=xs_, r0=r0: e.dma_start(out=X2.ap()[r0:r0 + 128, :], in_=x1l[xs_][:]), reads=[("x1l", xs_)], writes=[("X2", r0 // 128)])
                if nxt is not None:
                    trans6(nxt, b_)

        for t in range(min(2, NT6)):
            loads6(t)
        rms6(0)
        for b_ in range(BP6):
            trans6(0, b_)
        for t in range(NT6):
            back6(t, t + 1 if t + 1 < NT6 else None)
            if t + 2 < NT6:
                loads6(t + 2)
        S.barrier()

    FW = 256
    with ExitStack() as es:
        gff = SB(es, "gff", [128, D], F32)
        bcast_load(gff, g_ffn, D, "gff")
        H3T = SB(es, "H3T", [128, KC, T7], BF16)
        x2l = [SB(es, f"x2l{i}", [128, D], F32) for i in range(2)]
        hb7 = [SB(es, f"hb7{i}", [128, D], BF16) for i in range(2)]
        junk = SB(es, "junk7", [128, D], BF16)
        ss7 = [SB(es, f"ss7{i}", [128, 1], F32) for i in range(2)]
        rs7 = [SB(es, f"rs7{i}", [128, 1], F32) for i in range(2)]
        wg = [SB(es, f"wg{i}", [128, KC, FW], BF16) for i in range(2)]
        wu = [SB(es, f"wu{i}", [128, KC, FW], BF16) for i in range(2)]
        sg = [SB(es, f"sg{i}", [128, 512], F32) for i in range(2)]
        ast = [SB(es, f"ast{i}", [128, FW // 128, T7], BF16) for i in range(2)]
        ptr = [PS(es, f"ptr7_{i}", [128, D], BF16) for i in range(2)]
        pgt = [PS(es, f"pgt{i}", [128, 512]) for i in range(2)]
        put = [PS(es, f"put{i}", [128, 512]) for i in range(2)]
        w_fi_v = WFI_B.ap().rearrange("(k p) n -> p k n", p=128)
        xc = 0
        wc = 0
        pc_ = 0
        TW = min(512, T7)
        for t in range(TOWN // T7):
            for b in range(T7 // 128):
                sl = xc % 2
                xc += 1
                r0 = t * T7 + b * 128
                S.add("sp", lambda e, sl=sl, r0=r0: e.dma_start(out=x2l[sl][:], in_=X2.ap()[r0:r0 + 128, :]), writes=[("x2l", sl)])
                rms_block(x2l[sl][:], ("x2l", sl), gff, "gff", hb7[sl], ("hb7", sl), junk, ss7[sl], rs7[sl], ("p7", sl))
                transpose_block(hb7[sl], ("hb7", sl), ptr, H3T, "H3T", b * 128)
            for fc in range(DFF // FW):
                ws = wc % 2
                wc += 1
                S.add("pq", lambda e, ws=ws, fc=fc: e.dma_start(out=wg[ws][:], in_=w_fi_v[:, :, fc * FW:(fc + 1) * FW]), writes=[("wg", ws)])
                S.add("pq", lambda e, ws=ws, fc=fc: e.dma_start(out=wu[ws][:], in_=w_fi_v[:, :, DFF + fc * FW:DFF + (fc + 1) * FW]), writes=[("wu", ws)])
                for fb in range(FW // 128):
                    for tt in range(T7 // TW):
                        pi = pc_ % 2
                        pc_ += 1
                        for kc in range(KC):
                            S.add("pe", lambda e, pi=pi, ws=ws, kc=kc, fb=fb, tt=tt: e.matmul(pgt[pi][:, 0:TW], lhsT=wg[ws][:, kc, fb * 128:(fb + 1) * 128], rhs=H3T[:, kc, tt * TW:(tt + 1) * TW],
                                                                                         start=(kc == 0), stop=(kc == KC - 1)), reads=[("wg", ws), "H3T"], writes=[("pgt", pi)])
                        for kc in range(KC):
                            S.add("pe", lambda e, pi=pi, ws=ws, kc=kc, fb=fb, tt=tt: e.matmul(put[pi][:, 0:TW], lhsT=wu[ws][:, kc, fb * 128:(fb + 1) * 128], rhs=H3T[:, kc, tt * TW:(tt + 1) * TW],
                                                                                         start=(kc == 0), stop=(kc == KC - 1)), reads=[("wu", ws), "H3T"], writes=[("put", pi)])
                        S.add("act", lambda e, pi=pi: e.activation(out=sg[pi][:, 0:TW], in_=pgt[pi][:, 0:TW], func=AF.Silu), reads=[("pgt", pi)], writes=[("sg", pi)])
                        S.add("dve", lambda e, pi=pi, ws=ws, fb=fb, tt=tt: e.tensor_tensor(out=ast[ws][:, fb, tt * TW:(tt + 1) * TW], in0=put[pi][:, 0:TW], in1=sg[pi][:, 0:TW], op=ALU.mult),
                              reads=[("put", pi), ("sg", pi)], writes=[("ast", ws)])
                S.add("sp", lambda e, ws=ws, fc=fc, t=t: e.dma_start(out=dap(ACTD, fc * (FW // 128) * 128 * TOWN + t * T7, [[TOWN, 128], [128 * TOWN, FW // 128], [1, T7]]), in_=ast[ws][:]),
                      reads=[("ast", ws)], writes=[("ACTD", fc, t)])
        S.barrier()

    with ExitStack() as es:
        gfi = SB(es, "gfi", [128, D], F32)
        bcast_load(gfi, g_fin, D, "gfi")
        ACTT = SB(es, "ACTT", [128, FC, T8], BF16)
        wfo = [SB(es, f"wfo{i}", [128, FC, FW], BF16) for i in range(2)]
        xr = [SB(es, f"xr{i}", [128, FW], F32) for i in range(8)]
        x2l = [SB(es, f"x8l{i}", [128, D], F32) for i in range(2)]
        junk = SB(es, "junk8", [128, D], BF16)
        ss7 = [SB(es, f"ss8{i}", [128, 2], F32) for i in range(2)]
        pgt = [PS(es, f"pg8{i}", [128, 512]) for i in range(4)]
        w_fo_v = WFO_B.ap().rearrange("(k p) n -> p k n", p=128)
        NT8 = TOWN // T8
        NN = D // FW
        BP8 = T8 // 128
        items = [(t, n, b_) for t in range(NT8) for n in range(NN) for b_ in range(BP8)]
        xcnt = [0]

        def load_actt(t):
            for g4 in range(4):
                S.add("sp", lambda e, g4=g4, t=t: e.dma_start(out=ACTT[:, g4 * 11:(g4 + 1) * 11, :],
                                                               in_=dap(ACTD, g4 * 11 * 128 * TOWN + t * T8, [[TOWN, 128], [128 * TOWN, 11], [1, T8]])),
                      writes=[("ACTT", g4)])

        def load_wfo(ci):
            n = ci % NN
            S.add("pq", lambda e, ci=ci, n=n: e.dma_start(out=wfo[ci % 2][:], in_=w_fo_v[:, :, n * FW:(n + 1) * FW]), writes=[("wfo", ci % 2)])

        def load_xr(k):
            t, n, b_ = items[k]
            r0 = t * T8 + b_ * 128
            S.add("sp", lambda e, k=k, n=n, r0=r0: e.dma_start(out=xr[k % 8][:], in_=dap(X2, r0 * D + n * FW, [[D, 128], [1, FW]])), writes=[("xr", k % 8)])

        def comp8(k):
            t, n, b_ = items[k]
            ci = t * NN + n
            pi = k % 4
            r0 = t * T8 + b_ * 128
            for kc in range(FC):
                S.add("pe", lambda e, pi=pi, ci=ci, kc=kc, b_=b_: e.matmul(pgt[pi][:, 0:FW], lhsT=ACTT[:, kc, b_ * 128:(b_ + 1) * 128], rhs=wfo[ci % 2][:, kc, :],
                                                                       start=(kc == 0), stop=(kc == FC - 1)), reads=[("wfo", ci % 2), ("ACTT", kc // 11)], writes=[("pg8", pi)])
            xi = k % 8
            S.add("dve", lambda e, pi=pi, xi=xi: e.tensor_tensor(out=xr[xi][:], in0=pgt[pi][:, 0:FW], in1=xr[xi][:], op=ALU.add), reads=[("pg8", pi), ("xr", xi)], writes=[("xr", xi)])
            S.add("sp", lambda e, xi=xi, n=n, r0=r0: e.dma_start(out=dap(X2, r0 * D + n * FW, [[D, 128], [1, FW]]), in_=xr[xi][:]),
                  reads=[("xr", xi)], writes=[("X3", r0 // 128, n)])

        def final8(t, blocks=None):
            for b_ in (range(BP8) if blocks is None else blocks):
                sl = xcnt[0] % 2
                xcnt[0] += 1
                r0 = t * T8 + b_ * 128
                S.add("sp", lambda e, sl=sl, r0=r0: e.dma_start(out=x2l[sl][:], in_=X2.ap()[r0:r0 + 128, :]),
                      reads=[("X3", r0 // 128, n) for n in range(NN)], writes=[("x8l", sl)])
                S.add("act", lambda e, sl=sl: e.activation(out=junk[:], in_=x2l[sl][:], func=AF.Square, accum_out=ss7[sl][:, 0:1]),
                      reads=[("x8l", sl)], writes=[("ss8", sl)])
                S.add("act", lambda e, sl=sl: e.activation(out=ss7[sl][:, 1:2], in_=ss7[sl][:, 0:1], func=AF.Ln, bias=epsc[:, 0:1], scale=1.0 / D),
                      reads=[("ss8", sl), "epsc"], writes=[("ss8", sl)])
                S.add("act", lambda e, sl=sl: e.activation(out=ss7[sl][:, 1:2], in_=ss7[sl][:, 1:2], func=AF.Exp, scale=-0.5),
                      reads=[("ss8", sl)], writes=[("ss8", sl)])
                S.add("dve", lambda e, sl=sl: e.scalar_tensor_tensor(out=x2l[sl][:], in0=x2l[sl][:], scalar=ss7[sl][:, 1:2], in1=gfi[:], op0=ALU.mult, op1=ALU.mult),
                      reads=[("x8l", sl), ("ss8", sl), "gfi"], writes=[("x8l", sl)])
                S.add("sp", lambda e, sl=sl, r0=r0: e.dma_start(out=y_out.ap()[r0:r0 + 128, :], in_=x2l[sl][:]), reads=[("x8l", sl)], writes=[("y", r0 // 128)])

        load_actt(0)
        load_wfo(0)
        for k in range(min(6, len(items))):
            load_xr(k)
        pend_final = None
        pend_blocks = []
        for k, (t, n, b_) in enumerate(items):
            ci = t * NN + n
            if b_ == 0:
                if ci + 1 < NT8 * NN:
                    load_wfo(ci + 1)
            if b_ == BP8 // 2 and pend_final is not None and pend_blocks:
                nb_ = -(-BP8 // NN)
                final8(pend_final, pend_blocks[:nb_])
                pend_blocks = pend_blocks[nb_:]
            if k + 6 < len(items):
                load_xr(k + 6)
            comp8(k)
            if n == NN - 1 and b_ == BP8 - 1:
                if pend_final is not None and pend_blocks:
                    final8(pend_final, pend_blocks)
                if t + 1 < NT8:
                    load_actt(t + 1)
                pend_final = t
                pend_blocks = list(range(BP8))
        if pend_final is not None and pend_blocks:
            final8(pend_final, pend_blocks)
        S.barrier()

    print("ops", len(S.ops), "sbuf_rem", nc.sbuf_bytes_remaining)
    S.build(sem)
    top.close()
    return nc, LU, U0, QC


def _t5_bucket_np(rel):
    rel = np.asarray(rel, np.int32)
    half = 16
    max_exact = 8
    n = np.abs(rel)
    nf = np.maximum(n, 1).astype(np.float32)
    large = max_exact + (np.log(nf / np.float32(max_exact)) / np.float32(math.log(128 / max_exact))
                         * np.float32(half - max_exact)).astype(np.int32)
    large = np.minimum(large, half - 1)
    return np.where(rel > 0, half, 0) + np.where(n < max_exact, n, large)


def _structural(kind, LU, U0):
    j = np.arange(LU)
    bk = _t5_bucket_np(U0 - j)
    near = np.zeros((33, LU), np.float32)
    near[bk, j] = 1.0

    def const(b):
        a = np.zeros((33, LU), np.float32)
        a[b, :] = 1.0
        return a

    def selv(b):
        a = np.zeros((33, 128), np.float32)
        a[b, :] = 1.0
        return a
    if kind == "p0":
        hi, lo, oth = near, const(31), 31
    elif kind == "p1":
        hi, lo, oth = const(15), near, 15
    else:
        hi, lo, oth = const(32), const(32), 32
    oh_all = np.concatenate([near, hi, lo], axis=1)
    sel_all = np.concatenate([selv(15), selv(31), selv(oth)], axis=1)
    return np.ascontiguousarray(oh_all), np.ascontiguousarray(sel_all)


_WNAMES = ["rel_bias_table", "norm_mix_g", "w_in", "ln_v_g", "ln_v_b", "w_spatial", "b_spatial", "lambda_q1", "lambda_k1",
           "lambda_q2", "lambda_k2", "subln_g", "w_proj_a", "w_proj_b", "w_out", "norm_cross_g", "norm_mem_g", "w_cq", "w_ck",
           "w_cv", "w_co", "norm_ffn_g", "w_ffn_in", "w_ffn_out", "norm_final_g"]


def run_layer(inputs, n_prompt_cores=4):
    xp = np.asarray(inputs["x_prompt"], np.float32)
    xs = np.asarray(inputs["x_sample"], np.float32)
    mp = np.asarray(inputs["mem_prompt"], np.float32)
    msm = np.asarray(inputs["mem_sample"], np.float32)
    B, SEQ, _ = xp.shape
    DB, DSEQ, _ = xs.shape
    TOWN = SEQ // 2
    assert DSEQ == TOWN and B * 2 + DB == 8
    nc, LU, U0, QC = build_program(TOWN)
    wts = {}
    for k in _WNAMES:
        a = np.asarray(inputs[k], np.float32)
        if k != "norm_final_g" and k != "rel_bias_table":
            a = a[0]
        wts[k] = np.ascontiguousarray(a)
    in_maps = []
    for c in range(8):
        m = dict(wts)
        if c < 2 * B:
            b, half = c // 2, c % 2
            m["x_own"] = np.ascontiguousarray(xp[b, half * TOWN:(half + 1) * TOWN])
            m["x_oth"] = np.ascontiguousarray(xp[b, (1 - half) * TOWN:(2 - half) * TOWN])
            m["mem"] = np.ascontiguousarray(mp[b])
            oh, sel = _structural("p0" if half == 0 else "p1", LU, U0)
        else:
            b = c - 2 * B
            m["x_own"] = np.ascontiguousarray(xs[b])
            m["x_oth"] = m["x_own"]
            m["mem"] = np.ascontiguousarray(msm[b])
            oh, sel = _structural("s", LU, U0)
        m["oh_all"] = oh
        m["sel_all"] = sel
        in_maps.append(m)
    res = run_bass_kernel_spmd(nc, in_maps, core_ids=list(range(8)))
    yp = np.empty_like(xp)
    ys = np.empty_like(xs)
    for c in range(8):
        y = np.asarray(res.results[c]["y"], np.float32)
        if c < 2 * B:
            b, half = c // 2, c % 2
            yp[b, half * TOWN:(half + 1) * TOWN] = y
        else:
            ys[c - 2 * B] = y
    return yp, ys


def kernel(**inputs):
    return run_layer(inputs)
```

```python
import math
import numpy as np
from contextlib import ExitStack
import concourse.bass as bass
import concourse.mybir as mybir
from concourse.bass_utils import run_bass_kernel_spmd

F32 = mybir.dt.float32
BF16 = mybir.dt.bfloat16
AF = mybir.ActivationFunctionType
ALU = mybir.AluOpType
AX = mybir.AxisListType

COMPUTE = ("pe", "act", "dve", "pool")
NDSEM = 16

D = 2048
KC = 16
NIN = 14336
H = 8
DFF = 5632
FC = 44
NMEM = 256
CW = 512
EPS = 1e-6
LAM_INIT = 0.8 - 0.6 * math.exp(-0.3 * 0)
NEG = -30000.0


import types


def _freeze(fn):
    if fn is None or fn.__closure__ is None:
        return fn
    cells = []
    for c in fn.__closure__:
        try:
            v = c.cell_contents
            if isinstance(v, types.FunctionType):
                v = _freeze(v)
            cells.append(types.CellType(v))
        except ValueError:
            cells.append(c)
    return types.FunctionType(fn.__code__, fn.__globals__, fn.__name__, fn.__defaults__, tuple(cells))


class Op:
    __slots__ = ("eng", "fn", "deps", "signal", "sigval", "dslot", "dtarget", "ndma")

    def __init__(self, eng, fn):
        self.eng = eng
        self.fn = fn
        self.deps = []
        self.signal = False
        self.sigval = 0
        self.dslot = None
        self.dtarget = 0
        self.ndma = 1


class Sched:
    def __init__(self, nc):
        self.nc = nc
        self.engs = {"pe": nc.tensor, "act": nc.scalar, "dve": nc.vector, "pool": nc.gpsimd,
                     "sp": nc.sync, "pq": nc.gpsimd}
        self.stream = {"pe": "pe", "act": "act", "dve": "dve", "pool": "pool", "sp": "sp", "pq": "pool"}
        self.ops = []
        self.last_w = {}
        self.readers = {}
        self.waited = {}
        self.waited_d = {}
        self.idx = {}
        self.dma_count = {"sp": 0, "pq": 0}
        self.dma_slot_last = {}
        self.last_by_eng = {}

    def _reduce(self, op, st, deps):
        best = {}
        for d in deps:
            if d is op:
                continue
            de = d.eng
            if de in self.dma_count:
                if (st, id(d)) in self.waited_d:
                    continue
                best[("d", id(d))] = d
            else:
                if de == "pe" and st == "pe":
                    continue
                di = self.idx[id(d)]
                if self.waited.get((st, de), -1) >= di:
                    continue
                cur = best.get(("c", de))
                if cur is None or self.idx[id(cur)] < di:
                    best[("c", de)] = d
        for k, d in best.items():
            d.signal = True
            op.deps.append(d)
            if k[0] == "d":
                self.waited_d[(st, id(d))] = True
            else:
                self.waited[(st, d.eng)] = self.idx[id(d)]

    def add(self, eng, fn, reads=(), writes=(), ndma=1):
        op = Op(eng, _freeze(fn))
        op.ndma = ndma
        self.idx[id(op)] = len(self.ops)
        deps = []
        for k in reads:
            w = self.last_w.get(k)
            if w is not None:
                deps.append(w)
        for k in writes:
            w = self.last_w.get(k)
            if w is not None:
                deps.append(w)
            deps.extend(self.readers.get(k, {}).values())
        st = self.stream[eng]
        if eng in self.dma_count:
            c = self.dma_count[eng]
            self.dma_count[eng] = c + 1
            slot = (eng, c % NDSEM)
            prev = self.dma_slot_last.get(slot)
            if prev is not None:
                deps.append(prev)
            self.dma_slot_last[slot] = op
            op.dslot = slot
            op.dtarget = (prev.dtarget if prev is not None else 0) + 16 * ndma
        self._reduce(op, st, deps)
        self.ops.append(op)
        self.last_by_eng[eng] = op
        for k in reads:
            self.readers.setdefault(k, {})[st] = op
        for k in writes:
            self.last_w[k] = op
            self.readers[k] = {}
        return op

    def barrier(self, streams=("pe", "act", "dve", "pool", "sp")):
        prods = [d for e, d in self.last_by_eng.items() if e in COMPUTE]
        prods += list(self.dma_slot_last.values())
        for st in streams:
            op = Op(st, None)
            self.idx[id(op)] = len(self.ops)
            self._reduce(op, st, prods)
            self.ops.append(op)

    def build(self, sem):
        counts = {e: 0 for e in COMPUTE}
        for op in self.ops:
            if op.eng in COMPUTE and op.signal:
                counts[op.eng] += 1
                op.sigval = counts[op.eng]
        for op in self.ops:
            seng = self.engs[self.stream[op.eng]]
            for d in op.deps:
                if d.eng in COMPUTE:
                    seng.wait_ge(sem[d.eng], d.sigval)
                else:
                    seng.wait_ge(sem[d.dslot], d.dtarget)
            if op.fn is None:
                continue
            res = op.fn(self.engs[op.eng])
            if op.eng in COMPUTE:
                if op.signal:
                    res.then_inc(sem[op.eng], 1)
            else:
                insts = res if isinstance(res, (list, tuple)) else [res]
                assert len(insts) == op.ndma
                for ins in insts:
                    ins.then_inc(sem[op.dslot], 16)


def dap(t, off, dims):
    return bass.AP(t, off, [list(x) for x in dims])


def build_program(TOWN):
    TK = 2 * TOWN
    NOB = TOWN // 128
    NKB = TK // 128
    QC = min(512, TOWN)
    NQC = TOWN // QC
    QS = QC // 128
    U0 = QC + 255
    LU = ((2 * QC + 511 + 511) // 512) * 512
    T2 = min(1024, TOWN)
    T3 = min(256, TOWN)
    T7 = min(2048, TOWN)
    T8 = min(1024, TOWN)

    nc = bass.Bass("TRN2", target_bir_lowering=False)

    def din(name, shape, dt=F32):
        return nc.dram_tensor(name, list(shape), dt, kind="ExternalInput")

    def dscr(name, shape, dt):
        return nc.dram_tensor(name, list(shape), dt, kind="Internal")

    x_own = din("x_own", [TOWN, D]); x_oth = din("x_oth", [TOWN, D]); mem = din("mem", [NMEM, D])
    rel_tab = din("rel_bias_table", [32, H])
    g_mix = din("norm_mix_g", [D]); w_in = din("w_in", [D, NIN])
    ln_g = din("ln_v_g", [D]); ln_b = din("ln_v_b", [D])
    w_sp = din("w_spatial", [8, 128, 128]); b_sp = din("b_spatial", [8, 128])
    lq1 = din("lambda_q1", [128]); lk1 = din("lambda_k1", [128]); lq2 = din("lambda_q2", [128]); lk2 = din("lambda_k2", [128])
    sub_g = din("subln_g", [256])
    w_pa = din("w_proj_a", [D, D]); w_pb = din("w_proj_b", [D, D]); w_out = din("w_out", [D, D])
    g_cross = din("norm_cross_g", [D]); g_mem = din("norm_mem_g", [D])
    w_cq = din("w_cq", [D, 512]); w_ck = din("w_ck", [D, 512]); w_cv = din("w_cv", [D, 512]); w_co = din("w_co", [512, D])
    g_ffn = din("norm_ffn_g", [D]); w_fi = din("w_ffn_in", [D, 2 * DFF]); w_fo = din("w_ffn_out", [DFF, D])
    g_fin = din("norm_final_g", [D])
    oh_all = din("oh_all", [33, 3 * LU])
    sel_all = din("sel_all", [33, 3 * 128])
    y_out = nc.dram_tensor("y", [TOWN, D], F32, kind="ExternalOutput")

    QT = dscr("QT", [H * 2, 128, TOWN], BF16)
    KT = dscr("KT", [H * 2, 128, TK], BF16)
    VW = 288
    VA = dscr("VA", [H, 128, NKB, VW], BF16)
    GU = dscr("GU", [TOWN, D], BF16); GVA = dscr("GVA", [TOWN, D], BF16)
    SGA = dscr("SGA", [KC, 128, TOWN], BF16); SGB = dscr("SGB", [KC, 128, TOWN], BF16)
    A1T = dscr("A1T", [KC, 128, TOWN], BF16); MT = dscr("MT", [KC, 128, TOWN], BF16)
    OB = dscr("OB", [TOWN, D], BF16)
    X1 = dscr("X1", [TOWN, D], F32); X2 = dscr("X2", [TOWN, D], F32)
    UD = dscr("UD", [3, H, LU], F32)
    ACTD = dscr("ACTD", [FC, 128, TOWN], BF16)
    WPB_B = dscr("WPB_B", [D, D], BF16); WOUT_B = dscr("WOUT_B", [D, D], BF16)
    WCQ_B = dscr("WCQ_B", [D, 512], BF16); WCO_B = dscr("WCO_B", [512, D], BF16)
    WFI_B = dscr("WFI_B", [D, 2 * DFF], BF16); WFO_B = dscr("WFO_B", [DFF, D], BF16)

    S = Sched(nc)
    top = ExitStack()
    sem = {e: top.enter_context(nc.semaphore(f"s_{e}")) for e in COMPUTE}
    for q in ("sp", "pq"):
        for i in range(NDSEM):
            sem[(q, i)] = top.enter_context(nc.semaphore(f"d_{q}{i}"))

    def SB(es, name, shape, dt):
        return es.enter_context(nc.sbuf_tensor(name, list(shape), dt))

    def PS(es, name, shape, dt=F32):
        return es.enter_context(nc.psum_tensor(name, list(shape), dt))

    ident = SB(top, "ident", [128, 128], BF16)
    jrev = SB(top, "jrev", [128, 128], BF16)
    onesb = SB(top, "onesb", [128, 128], BF16)
    cls = SB(top, "cls", [128, 4, H], F32)
    neglam = SB(top, "neglam", [128, 1], F32)
    gsub = SB(top, "gsub", [128, 256], F32)
    kct = SB(top, "kct", [128, 4, NMEM], BF16)
    vc = SB(top, "vc", [128, 2, 512], BF16)
    epsc = SB(top, "epsc", [128, 2], F32)

    def bcast_load(es_tile, src, n, key):
        S.add("sp", lambda e: e.dma_start(out=es_tile[:], in_=dap(src, 0, [[0, 128], [1, n]])), writes=[key])

    def rms_block(xs_ap, xkey, gt, gkey, hb, hkey, junk, ss, rstd, tag, eps_col=0):
        S.add("act", lambda e: e.activation(out=junk[:], in_=xs_ap, func=AF.Square, accum_out=ss[:]),
              reads=[xkey], writes=[("junk", tag), ("ss", tag)])
        S.add("act", lambda e: e.activation(out=rstd[:], in_=ss[:], func=AF.Sqrt, bias=epsc[:, eps_col:eps_col + 1], scale=1.0 / D),
              reads=[("ss", tag), "epsc"], writes=[("rstd", tag)])
        S.add("dve", lambda e: e.reciprocal(rstd[:], rstd[:]), reads=[("rstd", tag)], writes=[("rstd", tag)])
        S.add("dve", lambda e: e.scalar_tensor_tensor(out=hb[:], in0=xs_ap, scalar=rstd[:, 0:1], in1=gt[:], op0=ALU.mult, op1=ALU.mult),
              reads=[xkey, ("rstd", tag), gkey], writes=[hkey])

    tcount = [0]

    def transpose_block(hb, hkey, ptr, dst, dkey, tok0, nch=KC):
        slot = tcount[0] % len(ptr)
        tcount[0] += 1
        p = ptr[slot]
        pk = ("ptr", slot)
        for kc in range(nch):
            S.add("pe", lambda e, kc=kc: e.transpose(p[:, kc * 128:(kc + 1) * 128], hb[:, kc * 128:(kc + 1) * 128], ident[:]),
                  reads=[hkey, "ident"], writes=[pk])
        eng = "act" if tcount[0] % 2 == 0 else "dve"
        src = p[:, 0:nch * 128].rearrange("p (k t) -> p k t", k=nch)
        if eng == "act":
            S.add("act", lambda e: e.copy(out=dst[:, 0:nch, tok0:tok0 + 128], in_=src), reads=[pk], writes=[dkey])
        else:
            S.add("dve", lambda e: e.tensor_copy(dst[:, 0:nch, tok0:tok0 + 128], src), reads=[pk], writes=[dkey])

    with ExitStack() as es:
        idf = SB(es, "idf", [128, 128], F32)
        S.add("pool", lambda e: e.memset(idf[:], 0.0), writes=["idf"])
        S.add("pool", lambda e: e.affine_select(out=idf[:], in_=idf[:], pattern=[[-1, 128]], compare_op=ALU.not_equal,
                                                fill=1.0, base=0, channel_multiplier=1), reads=["idf"], writes=["idf"])
        S.add("dve", lambda e: e.tensor_copy(ident[:], idf[:]), reads=["idf"], writes=["ident"])
        jf = SB(es, "jf", [128, 128], F32)
        S.add("pool", lambda e: e.memset(jf[:], 0.0), writes=["jf"])
        S.add("pool", lambda e: e.affine_select(out=jf[:], in_=jf[:], pattern=[[1, 128]], compare_op=ALU.not_equal,
                                                fill=1.0, base=-127, channel_multiplier=1), reads=["jf"], writes=["jf"])
        S.add("dve", lambda e: e.tensor_copy(jrev[:], jf[:]), reads=["jf"], writes=["jrev"])
        S.add("dve", lambda e: e.memset(onesb[:], 1.0), writes=["onesb"])
        S.add("dve", lambda e: e.memset(epsc[:, 0:1], EPS), writes=["epsc"])
        S.add("dve", lambda e: e.memset(epsc[:, 1:2], 1e-5), reads=["epsc"], writes=["epsc"])

        lv = SB(es, "lv", [128, 4, 128], F32)
        for i, t in enumerate((lq1, lk1, lq2, lk2)):
            S.add("sp", lambda e, i=i, t=t: e.dma_start(out=lv[:, i, :], in_=dap(t, 0, [[0, 128], [1, 128]])), writes=[("lv", i)])
        lp = SB(es, "lp", [128, 2, 128], F32)
        ls = SB(es, "ls", [128, 2], F32)
        for j in range(2):
            S.add("dve", lambda e, j=j: e.tensor_tensor(out=lp[:, j, :], in0=lv[:, 2 * j, :], in1=lv[:, 2 * j + 1, :], op=ALU.mult),
                  reads=[("lv", 2 * j), ("lv", 2 * j + 1)], writes=[("lp", j)])
            S.add("dve", lambda e, j=j: e.reduce_sum(out=ls[:, j:j + 1], in_=lp[:, j, :], axis=AX.X), reads=[("lp", j)], writes=[("ls", j)])
        le = SB(es, "le", [128, 2], F32)
        S.add("act", lambda e: e.activation(out=le[:], in_=ls[:], func=AF.Exp), reads=[("ls", 0), ("ls", 1)], writes=["le"])
        S.add("dve", lambda e: e.tensor_tensor(out=neglam[:], in0=le[:, 1:2], in1=le[:, 0:1], op=ALU.subtract), reads=["le"], writes=["neglam"])
        S.add("dve", lambda e: e.tensor_scalar_add(neglam[:], neglam[:], -LAM_INIT), reads=["neglam"], writes=["neglam"])
        S.add("sp", lambda e: e.dma_start(out=gsub[:], in_=dap(sub_g, 0, [[0, 128], [1, 256]])), writes=["gsub"])
        S.add("dve", lambda e: e.tensor_scalar_mul(gsub[:], gsub[:], 1.0 - LAM_INIT), reads=["gsub"], writes=["gsub"])

        tab = SB(es, "tab", [64, H], F32)
        S.add("dve", lambda e: e.memset(tab[32:64, :], NEG), writes=["tab"])
        S.add("sp", lambda e: e.dma_start(out=tab[0:32, :], in_=rel_tab.ap()), reads=["tab"], writes=["tab"])
        oh = SB(es, "oh", [33, 3 * LU], F32)
        S.add("sp", lambda e: e.dma_start(out=oh[:], in_=oh_all.ap()), writes=["oh"])
        sel = SB(es, "sel", [33, 3 * 128], F32)
        S.add("sp", lambda e: e.dma_start(out=sel[:], in_=sel_all.ap()), writes=["sel"])
        pu = PS(es, "pu", [128, 512])
        usb = SB(es, "usb", [H, 3 * LU], F32)
        for j in range(3 * LU // 512):
            S.add("pe", lambda e, j=j: e.matmul(pu[0:H, :], lhsT=tab[0:33, :], rhs=oh[:, j * 512:(j + 1) * 512], start=True, stop=True),
                  reads=["tab", "oh"], writes=["pu"])
            S.add("dve", lambda e, j=j: e.tensor_copy(usb[:, j * 512:(j + 1) * 512], pu[0:H, :]), reads=["pu"], writes=["usb"])
        S.add("sp", lambda e: e.dma_start(out=dap(UD, 0, [[LU, H], [H * LU, 3], [1, LU]]),
                                          in_=usb[:].rearrange("h (k l) -> h k l", k=3)), reads=["usb"], writes=["UD"])
        S.add("dve", lambda e: e.memset(cls[:], 0.0), writes=["cls"])
        for k in range(3):
            S.add("pe", lambda e, k=k: e.matmul(pu[:, 0:H], lhsT=sel[:, k * 128:(k + 1) * 128], rhs=tab[0:33, :], start=True, stop=True),
                  reads=["tab", "sel"], writes=["pu"])
            S.add("dve", lambda e, k=k: e.tensor_copy(cls[:, k, :], pu[:, 0:H]), reads=["pu", "cls"], writes=["cls"])

        gm = SB(es, "gm", [128, D], F32)
        bcast_load(gm, g_mem, D, "gm")
        memT = SB(es, "memT", [128, KC, NMEM], BF16)
        ptr = [PS(es, f"ptr{i}", [128, D], BF16) for i in range(2)]
        junk = SB(es, "junk0", [128, D], BF16)
        for b in range(2):
            ms = SB(es, f"ms{b}", [128, D], F32)
            hb = SB(es, f"mh{b}", [128, D], BF16)
            ss = SB(es, f"mss{b}", [128, 1], F32); rstd = SB(es, f"mrs{b}", [128, 1], F32)
            S.add("sp", lambda e, b=b, ms=ms: e.dma_start(out=ms[:], in_=mem.ap()[b * 128:(b + 1) * 128, :]), writes=[("ms", b)])
            rms_block(ms[:], ("ms", b), gm, "gm", hb, ("mh", b), junk, ss, rstd, ("m", b))
            transpose_block(hb, ("mh", b), ptr, memT, "memT", b * 128)
        wk = SB(es, "wk", [128, KC, 512], BF16)
        wv = SB(es, "wv", [128, KC, 512], BF16)
        S.add("pq", lambda e: e.dma_start(out=wk[:], in_=w_ck.ap().rearrange("(k p) n -> p k n", p=128)), writes=["wk"])
        S.add("pq", lambda e: e.dma_start(out=wv[:], in_=w_cv.ap().rearrange("(k p) n -> p k n", p=128)), writes=["wv"])
        pk0 = PS(es, "pk0", [128, 512])
        for hd in range(4):
            for kc in range(KC):
                S.add("pe", lambda e, hd=hd, kc=kc: e.matmul(pk0[:, 0:NMEM], lhsT=wk[:, kc, hd * 128:(hd + 1) * 128], rhs=memT[:, kc, :],
                                                           start=(kc == 0), stop=(kc == KC - 1)), reads=["wk", "memT"], writes=["pk0"])
            S.add("dve", lambda e, hd=hd: e.tensor_copy(kct[:, hd, :], pk0[:, 0:NMEM]), reads=["pk0"], writes=["kct"])
        for mb in range(2):
            for kc in range(KC):
                S.add("pe", lambda e, mb=mb, kc=kc: e.matmul(pk0[:, :], lhsT=memT[:, kc, mb * 128:(mb + 1) * 128], rhs=wv[:, kc, :],
                                                           start=(kc == 0), stop=(kc == KC - 1)), reads=["wv", "memT"], writes=["pk0"])
            S.add("dve", lambda e, mb=mb: e.tensor_copy(vc[:, mb, :], pk0[:, :]), reads=["pk0"], writes=["vc"])
        S.barrier()

    SCALE = 128 ** -0.5
    with ExitStack() as es:
        gmx = SB(es, "gmx", [128, D], F32)
        bcast_load(gmx, g_mix, D, "gmx")
        XTs = [SB(es, f"XT{i}", [128, KC, T2], BF16) for i in range(2)]
        xs = [SB(es, f"xs{i}", [128, D], F32) for i in range(2)]
        hbs = [SB(es, f"hb{i}", [128, D], BF16) for i in range(2)]
        junk = SB(es, "junk2", [128, D], BF16)
        sss = [SB(es, f"ss{i}", [128, 1], F32) for i in range(2)]
        rss = [SB(es, f"rs{i}", [128, 1], F32) for i in range(2)]
        wb = [SB(es, f"wb{i}", [128, KC, CW], BF16) for i in range(2)]
        ust = [SB(es, f"ust{i}", [128, T2 // 128, CW], BF16) for i in range(2)]
        vst = [SB(es, f"vst{i}", [128, 2, T2 // 128, VW], BF16) for i in range(2)]
        fst = [SB(es, f"fst{i}", [128, T2], BF16) for i in range(4)]
        ptr = [PS(es, f"ptr2_{i}", [128, D], BF16) for i in range(2)]
        pg = [PS(es, f"pg{i}", [128, 512]) for i in range(4)]
        for i in range(2):
            S.add("pool", lambda e, i=i: e.memset(vst[i][:], 1.0), writes=[("vst", i)])
        w_in_v = w_in.ap().rearrange("(k p) n -> p k n", p=128)
        cnt = {"w": 0, "pg": 0, "u": 0, "v": 0, "f": 0, "x": 0}
        tiles = [("own", t) for t in range(TOWN // T2)] + [("oth", t) for t in range(TOWN // T2)]
        def front_load(ti, b):
            kind_, t_ = tiles[ti]
            src_ = x_own if kind_ == "own" else x_oth
            sl = b % 2
            r0 = t_ * T2 + b * 128
            S.add("sp", lambda e, sl=sl, r0=r0, src_=src_: e.dma_start(out=xs[sl][:], in_=src_.ap()[r0:r0 + 128, :]), writes=[("xs", sl)])

        def front_compute(ti, b):
            sl = b % 2
            rms_block(xs[sl][:], ("xs", sl), gmx, "gmx", hbs[sl], ("hb", sl), junk, sss[sl], rss[sl], ("p2", sl))
            transpose_block(hbs[sl], ("hb", sl), ptr, XTs[ti % 2], ("XT", ti % 2), b * 128)

        NBF = T2 // 128
        front_load(0, 0)
        for b in range(NBF):
            if b + 1 < NBF:
                front_load(0, b + 1)
            front_compute(0, b)
        for ti, (kind, t) in enumerate(tiles):
            XT = XTs[ti % 2]
            xtk = ("XT", ti % 2)
            ktok0 = t * T2 + (0 if kind == "own" else TOWN)
            chunks = list(range(28)) if kind == "own" else list(range(12, 20))
            nl = 0
            ncp = 0
            for jpos, n in enumerate(chunks + [None]):
                if ti + 1 < len(tiles):
                    last = n is None
                    while nl < NBF and (nl <= jpos or last):
                        front_load(ti + 1, nl)
                        nl += 1
                        if last or True:
                            while ncp < nl - 1:
                                front_compute(ti + 1, ncp)
                                ncp += 1
                    if last:
                        while ncp < NBF:
                            front_compute(ti + 1, ncp)
                            ncp += 1
                if n is None:
                    break
                ws = cnt["w"] % 2
                cnt["w"] += 1
                S.add("pq", lambda e, ws=ws, n=n: e.dma_start(out=wb[ws][:], in_=w_in_v[:, :, n * CW:(n + 1) * CW]), writes=[("wb", ws)])
                if n < 8 or 16 <= n < 20:
                    if n < 8:
                        us = cnt["u"] % 2
                        cnt["u"] += 1
                    else:
                        vs = cnt["v"] % 2
                        cnt["v"] += 1
                    for tb in range(T2 // 128):
                        ps_i = cnt["pg"] % 4
                        cnt["pg"] += 1
                        p = pg[ps_i]
                        for kc in range(KC):
                            S.add("pe", lambda e, p=p, ws=ws, kc=kc, tb=tb: e.matmul(p[:, :], lhsT=XT[:, kc, tb * 128:(tb + 1) * 128], rhs=wb[ws][:, kc, :],
                                                                                  start=(kc == 0), stop=(kc == KC - 1)),
                                  reads=[xtk, ("wb", ws)], writes=[("pg", ps_i)])
                        if n < 8:
                            S.add("act", lambda e, p=p, us=us, tb=tb: e.activation(out=ust[us][:, tb, :], in_=p[:, :], func=AF.Gelu_apprx_tanh),
                                  reads=[("pg", ps_i)], writes=[("ust", us)])
                        else:
                            S.add("dve", lambda e, p=p, vs=vs, tb=tb: e.tensor_copy(vst[vs][:, :, tb, 0:256], p[:, :].rearrange("p (h e) -> p h e", h=2)),
                                  reads=[("pg", ps_i)], writes=[("vst", vs)])
                    if n < 8:
                        dst = GU if n < 4 else GVA
                        c0 = (n % 4) * CW
                        r0 = t * T2
                        S.add("sp", lambda e, us=us, dst=dst, c0=c0, r0=r0: e.dma_start(
                            out=dap(dst, r0 * D + c0, [[D, 128], [128 * D, T2 // 128], [1, CW]]), in_=ust[us][:]),
                            reads=[("ust", us)], writes=[("GU", n, t)])
                    else:
                        h0 = (n - 16) * 2
                        kb0 = ktok0 // 128
                        S.add("sp", lambda e, vs=vs, h0=h0, kb0=kb0: e.dma_start(
                            out=dap(VA, h0 * 128 * NKB * VW + kb0 * VW, [[NKB * VW, 128], [128 * NKB * VW, 2], [1, (T2 // 128) * VW]]),
                            in_=vst[vs][:].rearrange("p h k e -> p h (k e)")),
                            reads=[("vst", vs)], writes=[("VA", n, kind, t)])
                else:
                    for fb in range(4):
                        fs = cnt["f"] % 4
                        cnt["f"] += 1
                        for tt in range(T2 // 512 if T2 >= 512 else 1):
                            tw = min(512, T2)
                            ps_i = cnt["pg"] % 4
                            cnt["pg"] += 1
                            p = pg[ps_i]
                            for kc in range(KC):
                                S.add("pe", lambda e, p=p, ws=ws, kc=kc, fb=fb, tt=tt, tw=tw: e.matmul(
                                    p[:, 0:tw], lhsT=wb[ws][:, kc, fb * 128:(fb + 1) * 128], rhs=XT[:, kc, tt * tw:(tt + 1) * tw],
                                    start=(kc == 0), stop=(kc == KC - 1)), reads=[xtk, ("wb", ws)], writes=[("pg", ps_i)])
                            o_ap = lambda fs=fs, tt=tt, tw=tw: fst[fs][:, tt * tw:(tt + 1) * tw]
                            if 8 <= n < 12:
                                S.add("act", lambda e, p=p, o_ap=o_ap, tw=tw: e.mul(out=o_ap(), in_=p[:, 0:tw], mul=SCALE),
                                      reads=[("pg", ps_i)], writes=[("fst", fs)])
                            elif n < 16:
                                S.add("dve", lambda e, p=p, o_ap=o_ap, tw=tw: e.tensor_copy(o_ap(), p[:, 0:tw]),
                                      reads=[("pg", ps_i)], writes=[("fst", fs)])
                            else:
                                S.add("act", lambda e, p=p, o_ap=o_ap, tw=tw: e.activation(out=o_ap(), in_=p[:, 0:tw], func=AF.Sigmoid),
                                      reads=[("pg", ps_i)], writes=[("fst", fs)])
                        if 8 <= n < 12:
                            idx = (n - 8) * 4 + fb
                            dd = dap(QT, idx * 128 * TOWN + t * T2, [[TOWN, 128], [1, T2]])
                        elif n < 16:
                            idx = (n - 12) * 4 + fb
                            dd = dap(KT, idx * 128 * TK + ktok0, [[TK, 128], [1, T2]])
                        elif n < 24:
                            idx = (n - 20) * 4 + fb
                            dd = dap(SGA, idx * 128 * TOWN + t * T2, [[TOWN, 128], [1, T2]])
                        else:
                            idx = (n - 24) * 4 + fb
                            dd = dap(SGB, idx * 128 * TOWN + t * T2, [[TOWN, 128], [1, T2]])
                        S.add("sp", lambda e, fs=fs, dd=dd: e.dma_start(out=dd, in_=fst[fs][:]), reads=[("fst", fs)], writes=[("F", n, fb, kind, t)])
        S.barrier()

    with ExitStack() as es:
        wpa = SB(es, "wpa", [128, KC, D], BF16)
        for j in range(4):
            S.add("pq", lambda e, j=j: e.dma_start(out=wpa[:, :, j * 512:(j + 1) * 512],
                                                    in_=w_pa.ap().rearrange("(k p) n -> p k n", p=128)[:, :, j * 512:(j + 1) * 512]), writes=[("wpa", j)])
        wpa_keys = [("wpa", j) for j in range(4)]
        lng = SB(es, "lng", [128, D], F32); lnb = SB(es, "lnb", [128, D], F32)
        bcast_load(lng, ln_g, D, "lng"); bcast_load(lnb, ln_b, D, "lnb")
        wsb = SB(es, "wsb", [128, 8, 128], BF16)
        S.add("pq", lambda e: e.dma_start(out=wsb[:], in_=w_sp.ap().rearrange("g i j -> i g j")), writes=["wsb"])
        wsT = SB(es, "wsT", [128, 8, 128], BF16)
        ptr = [PS(es, "ptr3_0", [128, D], BF16)]
        for g in range(8):
            S.add("pe", lambda e, g=g: e.transpose(ptr[0][:, g * 128:(g + 1) * 128], wsb[:, g, :], ident[:]), reads=["wsb", "ident"], writes=[("ptr", 0)])
        S.add("dve", lambda e: e.tensor_copy(wsT[:], ptr[0][:, 0:1024].rearrange("p (g i) -> p g i", g=8)), reads=[("ptr", 0)], writes=["wsT"])
        bsT = SB(es, "bsT", [128, 8], F32)
        S.add("sp", lambda e: e.dma_start(out=bsT[:], in_=dap(b_sp, 0, [[1, 128], [128, 8]]), allow_slow_non_contiguous=True), writes=["bsT"])
        OATs = [SB(es, f"OAT{i}", [128, KC, T3], BF16) for i in range(2)]
        gub = [SB(es, f"gub{i}", [128, D], BF16) for i in range(4)]
        gvb = [SB(es, f"gvb{i}", [128, D], BF16) for i in range(4)]
        t1s = [SB(es, f"t1_{i}", [128, D], F32) for i in range(2)]
        vns = [SB(es, f"vn_{i}", [128, D], BF16) for i in range(2)]
        oab = [SB(es, f"oab{i}", [128, D], BF16) for i in range(2)]
        junk = SB(es, "junk3", [128, D], BF16)
        sts = [SB(es, f"st3_{i}", [128, 8], F32) for i in range(2)]
        pm = PS(es, "pm", [128, D])
        pgm = [PS(es, f"pg3_{i}", [128, 512]) for i in range(2)]
        sgas = [SB(es, f"sga{i}", [128, KC, T3], BF16) for i in range(2)]
        a1ss = [SB(es, f"a1s{i}", [128, KC, T3], BF16) for i in range(2)]
        NB3 = TOWN // 128
        BPT = T3 // 128
        NT3 = TOWN // T3
        pcnt = [0]

        def load_blk(i):
            s3 = i % 4
            r0 = i * 128
            S.add("sp", lambda e, s3=s3, r0=r0: e.dma_start(out=gub[s3][:], in_=GU.ap()[r0:r0 + 128, :]), writes=[("gub", s3)])
            S.add("sp", lambda e, s3=s3, r0=r0: e.dma_start(out=gvb[s3][:], in_=GVA.ap()[r0:r0 + 128, :]), writes=[("gvb", s3)])

        def load_sga(t):
            S.add("sp", lambda e, t=t: e.dma_start(out=sgas[t % 2][:], in_=dap(SGA, t * T3, [[TOWN, 128], [128 * TOWN, KC], [1, T3]])), writes=[("sga", t % 2)])

        def stage_a_head(i):
            s4 = i % 4
            sl = i % 2
            st = sts[sl]
            S.add("act", lambda e: e.activation(out=junk[:], in_=gvb[s4][:], func=AF.Identity, accum_out=st[:, 0:1]),
                  reads=[("gvb", s4)], writes=[("st", sl, 0)])
            S.add("act", lambda e: e.activation(out=junk[:], in_=gvb[s4][:], func=AF.Square, accum_out=st[:, 1:2]),
                  reads=[("gvb", s4)], writes=[("st", sl, 1)])
            S.add("dve", lambda e: e.tensor_scalar_mul(st[:, 2:4], st[:, 0:2], 1.0 / D), reads=[("st", sl, 0), ("st", sl, 1)], writes=[("st", sl, 2)])
            S.add("dve", lambda e: e.tensor_tensor(out=st[:, 4:5], in0=st[:, 2:3], in1=st[:, 2:3], op=ALU.mult), reads=[("st", sl, 2)], writes=[("st", sl, 4)])
            S.add("dve", lambda e: e.tensor_tensor(out=st[:, 5:6], in0=st[:, 3:4], in1=st[:, 4:5], op=ALU.subtract), reads=[("st", sl, 2), ("st", sl, 4)], writes=[("st", sl, 5)])
            S.add("act", lambda e: e.activation(out=st[:, 6:7], in_=st[:, 5:6], func=AF.Ln, bias=epsc[:, 0:1], scale=1.0),
                  reads=[("st", sl, 5), "epsc"], writes=[("st", sl, 6)])
            S.add("act", lambda e: e.activation(out=st[:, 7:8], in_=st[:, 6:7], func=AF.Exp, scale=-0.5), reads=[("st", sl, 6)], writes=[("st", sl, 7)])
            S.add("dve", lambda e: e.tensor_scalar(out=st[:, 4:5], in0=st[:, 2:3], scalar1=st[:, 7:8], scalar2=-1.0, op0=ALU.mult, op1=ALU.mult),
                  reads=[("st", sl, 2), ("st", sl, 7), ("st", sl, 5)], writes=[("st", sl, 4)])

        def stage_a_tail1(i):
            s4 = i % 4
            sl = i % 2
            t1 = t1s[sl]; st = sts[sl]
            S.add("act", lambda e: e.activation(out=t1[:], in_=gvb[s4][:], func=AF.Identity, bias=st[:, 4:5], scale=st[:, 7:8]),
                  reads=[("gvb", s4), ("st", sl, 4), ("st", sl, 7)], writes=[("t1", sl)])
            S.add("pool", lambda e: e.tensor_tensor(out=t1[:], in0=t1[:], in1=lng[:], op=ALU.mult), reads=[("t1", sl), "lng"], writes=[("t1", sl)])

        def stage_a_tail2(i):
            sl = i % 2
            t1 = t1s[sl]; vn = vns[sl]
            S.add("dve", lambda e: e.tensor_tensor(out=vn[:], in0=t1[:], in1=lnb[:], op=ALU.add), reads=[("t1", sl), "lnb"], writes=[("vn", sl)])

        def stage_b1(i):
            s3 = i % 4
            sl = i % 2
            vn = vns[sl]
            for g in range(8):
                S.add("pe", lambda e, g=g: e.matmul(pm[:, g * 256:(g + 1) * 256], lhsT=wsT[:, g, :], rhs=vn[:, g * 256:(g + 1) * 256], start=True, stop=True),
                      reads=["wsT", ("vn", sl)], writes=[("pm", g)])
            for g in range(8):
                S.add("dve", lambda e, g=g: e.scalar_tensor_tensor(out=oab[sl][:, g * 256:(g + 1) * 256], in0=pm[:, g * 256:(g + 1) * 256],
                                                                  scalar=bsT[:, g:g + 1], in1=gub[s3][:, g * 256:(g + 1) * 256], op0=ALU.add, op1=ALU.mult),
                      reads=[("pm", g), "bsT", ("gub", s3)], writes=[("oab", sl)])

        def stage_b2(i):
            sl = i % 2
            t = i // BPT
            transpose_block(oab[sl], ("oab", sl), ptr, OATs[t % 2], ("OAT", t % 2), (i % BPT) * 128)

        def gemm(t, fbs, last):
            OAT = OATs[t % 2]; sga = sgas[t % 2]; a1s = a1ss[t % 2]
            for fb in fbs:
                pi = pcnt[0] % 2
                pcnt[0] += 1
                for kc in range(KC):
                    S.add("pe", lambda e, pi=pi, kc=kc, fb=fb: e.matmul(pgm[pi][:, 0:T3], lhsT=wpa[:, kc, fb * 128:(fb + 1) * 128], rhs=OAT[:, kc, :],
                                                                     start=(kc == 0), stop=(kc == KC - 1)), reads=wpa_keys + [("OAT", t % 2)], writes=[("pg3", pi)])
                S.add("dve", lambda e, pi=pi, fb=fb: e.tensor_tensor(out=a1s[:, fb, :], in0=pgm[pi][:, 0:T3], in1=sga[:, fb, :], op=ALU.mult),
                      reads=[("pg3", pi), ("sga", t % 2)], writes=[("a1s", t % 2)])
            if last:
                S.add("sp", lambda e, t=t: e.dma_start(out=dap(A1T, t * T3, [[TOWN, 128], [128 * TOWN, KC], [1, T3]]), in_=a1s[:]),
                      reads=[("a1s", t % 2)], writes=[("A1T", t)])
                if t + 2 < NT3:
                    load_sga(t + 2)

        for i in range(min(4, NB3)):
            load_blk(i)
        for t in range(min(2, NT3)):
            load_sga(t)
        stage_a_head(0)
        stage_a_tail1(0)
        stage_a_tail2(0)
        if NB3 > 1:
            stage_a_head(1)
        gq = []
        NPIECE = BPT
        for i in range(NB3):
            if i + 2 < NB3:
                stage_a_head(i + 2)
            if i + 1 < NB3:
                stage_a_tail1(i + 1)
            stage_b1(i)
            if gq:
                gemm(*gq.pop(0))
            stage_b2(i)
            if i + 1 < NB3:
                stage_a_tail2(i + 1)
            if i + 4 < NB3:
                load_blk(i + 4)
            if (i + 1) % BPT == 0:
                t = i // BPT
                per = KC // NPIECE
                for pz in range(NPIECE):
                    gq.append((t, list(range(pz * per, (pz + 1) * per if pz < NPIECE - 1 else KC)), pz == NPIECE - 1))
        while gq:
            gemm(*gq.pop(0))
        S.barrier()

    with ExitStack() as es:
        kt = [SB(es, f"kt{m}", [128, TK], BF16) for m in range(2)]
        va = SB(es, "va", [128, NKB, VW], BF16)
        hbt = SB(es, "hbt", [128, 12, QC], BF16)
        qt = [SB(es, f"qt{i}", [128, QC], BF16) for i in range(4)]
        pt = [SB(es, f"pt{i}", [128, 2, QC], BF16) for i in range(3)]
        o1s = SB(es, "o1s", [128, QS, 257], F32)
        o2s = SB(es, "o2s", [128, QS, 257], F32)
        junkf = SB(es, "junkf4", [128, 256], F32)
        osb = SB(es, "osb", [128, QS, 256], F32)
        obst = [SB(es, f"obst{i}", [128, QS, 256], BF16) for i in range(2)]
        sm = SB(es, "sm4", [128, 4 * QS], F32)
        junk = SB(es, "junk4", [128, 256], BF16)
        spair = [PS(es, f"sp{i}", [128, 2, 512]) for i in range(2)]
        acc = [PS(es, f"acc{i}", [128, 512]) for i in range(QS)]
        deltas = [-256 + 128 * i for i in range(QS + 4)]
        ND = len(deltas)
        qcount = 0
        pcount = 0
        ocount = 0
        pending = []
        for h in range(H):
            NPC = 4 if NKB % 4 == 0 else 1
            KPB = NKB // NPC
            for m in range(2):
                for pc4 in range(NPC):
                    S.add("sp", lambda e, h=h, m=m, pc4=pc4: e.dma_start(out=kt[m][:, pc4 * KPB * 128:(pc4 + 1) * KPB * 128],
                                                                         in_=dap(KT, (2 * h + m) * 128 * TK + pc4 * KPB * 128, [[TK, 128], [1, KPB * 128]])),
                          writes=[("kt", m, pc4)])
                if m == 0:
                    for pc4 in range(NPC):
                        S.add("sp", lambda e, h=h, pc4=pc4: e.dma_start(out=va[:, pc4 * KPB:(pc4 + 1) * KPB, :],
                                                                       in_=dap(VA, h * 128 * NKB * VW + pc4 * KPB * VW, [[NKB * VW, 128], [1, KPB * VW]])),
                              writes=[("va", pc4)])
            for i, dl in enumerate(deltas):
                off = U0 - dl - 127
                S.add("pq", lambda e, i=i, off=off, h=h: e.dma_start(out=hbt[:, i, :], in_=dap(UD, (0 * H + h) * LU + off, [[1, 128], [1, QC]])),
                      reads=["UD"], writes=[("hbt", i)])
            for i, dl in enumerate((QC, QC + 128)):
                off = U0 - dl - 127
                S.add("pq", lambda e, i=i, off=off, h=h: e.dma_start(out=hbt[:, ND + i, :], in_=dap(UD, (1 * H + h) * LU + off, [[1, 128], [1, QC]])),
                      reads=["UD"], writes=[("hbt", ND + i)])
            for i, dl in enumerate((-256, -128)):
                off = U0 - dl - 127
                S.add("pq", lambda e, i=i, off=off, h=h: e.dma_start(out=hbt[:, ND + 2 + i, :], in_=dap(UD, (2 * H + h) * LU + off, [[1, 128], [1, QC]])),
                      reads=["UD"], writes=[("hbt", ND + 2 + i)])

            if h == 0:
                for srcw, dstw, rows in ((w_pb, WPB_B, D), (w_out, WOUT_B, D), (w_cq, WCQ_B, D), (w_co, WCO_B, 512), (w_fi, WFI_B, D), (w_fo, WFO_B, DFF)):
                    npc = 4
                    rr = rows // npc
                    for pc4 in range(npc):
                        S.add("pq", lambda e, srcw=srcw, dstw=dstw, pc4=pc4, rr=rr: e.dma_start(out=dstw.ap()[pc4 * rr:(pc4 + 1) * rr, :], in_=srcw.ap()[pc4 * rr:(pc4 + 1) * rr, :]),
                              writes=[("wconv", dstw.name, pc4)])
            for c in range(NQC):
                for m in range(2):
                    u_ = (h * NQC + c) * 2 + m
                    qs_ = u_ % 4
                    for ua in ((u_, u_ + 1, u_ + 2) if u_ == 0 else (u_ + 2,)):
                        if ua < H * NQC * 2:
                            h2, c2, m2 = ua // (2 * NQC), (ua // 2) % NQC, ua % 2
                            S.add("sp", lambda e, ua=ua, h2=h2, m2=m2, c2=c2: e.dma_start(out=qt[ua % 4][:], in_=dap(QT, (2 * h2 + m2) * 128 * TOWN + c2 * QC, [[TOWN, 128], [1, QC]])),
                                  writes=[("qt", ua % 4)])

                    def pair_info(kp):
                        tiles = []
                        for j in range(2):
                            kb = 2 * kp + j
                            if kb < NOB:
                                dlt = kb * 128 - c * QC
                                if -256 <= dlt <= QC + 128:
                                    tiles.append(deltas.index(dlt))
                                else:
                                    tiles.append("LO" if dlt < 0 else "HI")
                            else:
                                ko = kb - NOB
                                if c == NQC - 1 and ko < 2:
                                    tiles.append(ND + ko)
                                elif c == 0 and ko >= NOB - 2:
                                    tiles.append(ND + 2 + (ko - (NOB - 2)))
                                else:
                                    tiles.append("OTH")
                        return tiles

                    def emit_qk(kp):
                        sl = kp % 2
                        tiles = pair_info(kp)
                        use_mm = any(isinstance(x, int) for x in tiles)
                        for j in range(2):
                            kb = 2 * kp + j
                            S.add("pe", lambda e, sl=sl, j=j, kb=kb, qs_=qs_, use_mm=use_mm: e.matmul(
                                spair[sl][:, j, 0:QC], lhsT=kt[m][:, kb * 128:(kb + 1) * 128], rhs=qt[qs_][:, :], start=True, stop=not use_mm),
                                reads=[("kt", m, kb // KPB), ("qt", qs_)], writes=[("spair", sl)])
                            if use_mm:
                                ti = tiles[j]
                                assert isinstance(ti, int), "mixed near/far pair not supported"
                                S.add("pe", lambda e, sl=sl, j=j, ti=ti: e.matmul(spair[sl][:, j, 0:QC], lhsT=jrev[:], rhs=hbt[:, ti, :], start=False, stop=True),
                                      reads=["jrev", ("hbt", ti)], writes=[("spair", sl)])
                        if use_mm:
                            ci = 3
                        else:
                            assert tiles[0] == tiles[1]
                            ci = {"LO": 0, "HI": 1, "OTH": 2}[tiles[0]]
                        return ci

                    def emit_exp(kp, ci, ps_):
                        sl = kp % 2
                        S.add("act", lambda e, sl=sl, ps_=ps_, ci=ci: e.activation(out=pt[ps_][:, :, :], in_=spair[sl][:, :, 0:QC], func=AF.Exp,
                                                                              bias=cls[:, ci, h:h + 1], scale=1.0),
                              reads=[("spair", sl), "cls"], writes=[("pt", ps_)])

                    def emit_pv(kp, ps_):
                        for j in range(2):
                            kb = 2 * kp + j
                            for q in range(QS):
                                S.add("pe", lambda e, ps_=ps_, j=j, kb=kb, q=q: e.matmul(acc[q][:, 0:257], lhsT=pt[ps_][:, j, q * 128:(q + 1) * 128], rhs=va[:, kb, 0:257],
                                                                                    start=(kb == 0), stop=(kb == NKB - 1)),
                                      reads=[("pt", ps_), ("va", kb // KPB)], writes=[("acc", q)])

                    NP = NKB // 2
                    pend = None
                    for kp in range(NP):
                        if pending and kp == min(2, NP - 1) and pending[0][0] is not None:
                            pending[0][0]()
                            pending[0][0] = None
                        if pending and kp == min(14, NP - 1) and pending[0][0] is None:
                            pending.pop(0)[1]()
                        ci = emit_qk(kp)
                        ps_ = pcount % 3
                        pcount += 1
                        emit_exp(kp, ci, ps_)
                        if pend is not None:
                            emit_pv(*pend)
                        pend = (kp, ps_)
                    emit_pv(*pend)

                    if m == 0:
                        for q in range(QS):
                            S.add("dve", lambda e, q=q: e.tensor_copy(o1s[:, q, :], acc[q][:, 0:257]), reads=[("acc", q)], writes=[("o1s", q)])
                    else:
                        ob_ = ocount % 2
                        ocount += 1
                        for q in range(QS):
                            S.add("dve", lambda e, q=q: e.tensor_copy(o2s[:, q, :], acc[q][:, 0:257]), reads=[("acc", q)], writes=[("o2s", q)])

                        def part1():
                            for q in range(QS):
                                S.add("dve", lambda e, q=q: e.reciprocal(sm[:, q:q + 1], o1s[:, q, 256:257]), reads=[("o1s", q)], writes=[("sm", q)])
                                S.add("dve", lambda e, q=q: e.reciprocal(sm[:, QS + q:QS + q + 1], o2s[:, q, 256:257]), reads=[("o2s", q)], writes=[("sm", QS + q)])
                                S.add("dve", lambda e, q=q: e.tensor_tensor(out=sm[:, QS + q:QS + q + 1], in0=sm[:, QS + q:QS + q + 1], in1=neglam[:], op=ALU.mult),
                                      reads=[("sm", QS + q), "neglam"], writes=[("sm", QS + q)])
                                S.add("dve", lambda e, q=q: e.tensor_scalar_mul(osb[:, q, :], o1s[:, q, 0:256], sm[:, q:q + 1]),
                                      reads=[("o1s", q), ("sm", q)], writes=[("osb", q)])
                                S.add("dve", lambda e, q=q: e.scalar_tensor_tensor(out=osb[:, q, :], in0=o2s[:, q, 0:256], scalar=sm[:, QS + q:QS + q + 1], in1=osb[:, q, :],
                                                                                  op0=ALU.mult, op1=ALU.add),
                                      reads=[("o2s", q), ("sm", QS + q), ("osb", q)], writes=[("osb", q)])
                                S.add("dve", lambda e, q=q: e.scalar_tensor_tensor(out=junkf[:], in0=osb[:, q, :], scalar=1.0, in1=osb[:, q, :], op0=ALU.mult, op1=ALU.mult,
                                                                                  accum_out=sm[:, 2 * QS + q:2 * QS + q + 1]),
                                      reads=[("osb", q)], writes=["junk4", ("sm", 2 * QS + q)])

                        def part2(ob_=ob_, c=c, h=h):
                            S.add("act", lambda e: e.activation(out=sm[:, 3 * QS:4 * QS], in_=sm[:, 2 * QS:3 * QS], func=AF.Ln, bias=epsc[:, 1:2], scale=1.0 / 256),
                                  reads=[("sm", 2 * QS + q) for q in range(QS)] + ["epsc"], writes=[("sm", 3 * QS + q) for q in range(QS)])
                            S.add("act", lambda e: e.activation(out=sm[:, 3 * QS:4 * QS], in_=sm[:, 3 * QS:4 * QS], func=AF.Exp, scale=-0.5),
                                  reads=[("sm", 3 * QS + q) for q in range(QS)], writes=[("sm", 3 * QS + q) for q in range(QS)])
                            for q in range(QS):
                                S.add("dve", lambda e, q=q: e.scalar_tensor_tensor(out=obst[ob_][:, q, :], in0=osb[:, q, :], scalar=sm[:, 3 * QS + q:3 * QS + q + 1],
                                                                                  in1=gsub[:], op0=ALU.mult, op1=ALU.mult),
                                      reads=[("osb", q), ("sm", 3 * QS + q), "gsub"], writes=[("obst", ob_)])
                            S.add("sp", lambda e: e.dma_start(out=dap(OB, c * QC * D + h * 256, [[D, 128], [128 * D, QS], [1, 256]]), in_=obst[ob_][:]),
                                  reads=[("obst", ob_)], writes=[("OB", c, h)])
                        pending.append([part1, part2])
        while pending:
            p1, p2 = pending.pop(0)
            if p1 is not None:
                p1()
            p2()
        S.barrier()

    T5 = min(512, TOWN)
    T5A = min(256, TOWN)
    with ExitStack() as es:
        wpb = SB(es, "wpb", [128, KC, D], BF16)
        for j in range(4):
            S.add("pq", lambda e, j=j: e.dma_start(out=wpb[:, :, j * 512:(j + 1) * 512],
                                                    in_=WPB_B.ap().rearrange("(k p) n -> p k n", p=128)[:, :, j * 512:(j + 1) * 512]), writes=[("wpb", j)])
        wkeys = [("wpb", j) for j in range(4)]
        BPT5 = T5A // 128
        obl = [SB(es, f"obl{i}", [128, D], BF16) for i in range(2 * BPT5)]
        OBTs = [SB(es, f"OBT{i}", [128, KC, T5A], BF16) for i in range(2)]
        sgbs = [SB(es, f"sgb{i}", [128, KC, T5A], BF16) for i in range(2)]
        a1ls = [SB(es, f"a1l{i}", [128, KC, T5A], BF16) for i in range(2)]
        msts = [SB(es, f"mst{i}", [128, KC, T5A], BF16) for i in range(2)]
        tmp = [SB(es, f"tmp5{i}", [128, T5A], F32) for i in range(2)]
        ptr = [PS(es, f"ptr5_{i}", [128, D], BF16) for i in range(2)]
        pg5 = [PS(es, f"pg5_{i}", [128, 512]) for i in range(4)]
        NT5 = TOWN // T5A
        pcnt = [0]

        def loads5(t):
            ts_ = t % 2
            S.add("sp", lambda e, t=t: e.dma_start(out=sgbs[ts_][:], in_=dap(SGB, t * T5A, [[TOWN, 128], [128 * TOWN, KC], [1, T5A]])), writes=[("sgb", ts_)])
            S.add("sp", lambda e, t=t: e.dma_start(out=a1ls[ts_][:], in_=dap(A1T, t * T5A, [[TOWN, 128], [128 * TOWN, KC], [1, T5A]])), writes=[("a1l", ts_)])
            for b_ in range(BPT5):
                r0 = t * T5A + b_ * 128
                os_ = ts_ * BPT5 + b_
                S.add("sp", lambda e, os_=os_, r0=r0: e.dma_start(out=obl[os_][:], in_=OB.ap()[r0:r0 + 128, :]), writes=[("obl", os_)])

        def trans5(t):
            ts_ = t % 2
            for b_ in range(BPT5):
                os_ = ts_ * BPT5 + b_
                transpose_block(obl[os_], ("obl", os_), ptr, OBTs[ts_], ("OBT", ts_), b_ * 128)

        def gemm5(t):
            ts_ = t % 2
            OBT = OBTs[ts_]; sgb = sgbs[ts_]; a1l = a1ls[ts_]; mst = msts[ts_]
            for fb in range(KC):
                pi = pcnt[0] % 4
                pcnt[0] += 1
                for kc in range(KC):
                    S.add("pe", lambda e, pi=pi, kc=kc, fb=fb: e.matmul(pg5[pi][:, 0:T5A], lhsT=wpb[:, kc, fb * 128:(fb + 1) * 128], rhs=OBT[:, kc, :],
                                                                     start=(kc == 0), stop=(kc == KC - 1)), reads=wkeys + [("OBT", ts_)], writes=[("pg5", pi)])
                S.add("dve", lambda e, pi=pi, fb=fb: e.tensor_tensor(out=tmp[pi % 2][:], in0=pg5[pi][:, 0:T5A], in1=sgb[:, fb, :], op=ALU.mult),
                      reads=[("pg5", pi), ("sgb", ts_)], writes=[("tmp5", pi % 2)])
                S.add("pool", lambda e, pi=pi, fb=fb: e.tensor_tensor(out=mst[:, fb, :], in0=tmp[pi % 2][:], in1=a1l[:, fb, :], op=ALU.add),
                      reads=[("tmp5", pi % 2), ("a1l", ts_)], writes=[("mst", ts_)])
            S.add("sp", lambda e, t=t: e.dma_start(out=dap(MT, t * T5A, [[TOWN, 128], [128 * TOWN, KC], [1, T5A]]), in_=mst[:]),
                  reads=[("mst", ts_)], writes=[("MT", t)])

        for t in range(min(2, NT5)):
            loads5(t)
        trans5(0)
        for t in range(NT5):
            if t + 1 < NT5:
                trans5(t + 1)
            gemm5(t)
            if t + 2 < NT5:
                loads5(t + 2)
        S.barrier()

    with ExitStack() as es:
        wo = SB(es, "wo", [128, KC, D], BF16)
        for j in range(4):
            S.add("pq", lambda e, j=j: e.dma_start(out=wo[:, :, j * 512:(j + 1) * 512],
                                                    in_=WOUT_B.ap().rearrange("(k p) n -> p k n", p=128)[:, :, j * 512:(j + 1) * 512]), writes=[("wo", j)])
        wkeys = [("wo", j) for j in range(4)]
        mtl = [SB(es, f"mtl{i}", [128, KC, T5], BF16) for i in range(2)]
        xl = [SB(es, f"xl{i}", [128, D], F32) for i in range(3)]
        pg5 = [PS(es, f"pg5b_{i}", [128, 512]) for i in range(4)]
        NT5B = TOWN // T5
        BPB = T5 // 128
        NG = TOWN // 128
        pcnt = [0]

        def load_mt(t):
            S.add("sp", lambda e, t=t: e.dma_start(out=mtl[t % 2][:], in_=dap(MT, t * T5, [[TOWN, 128], [128 * TOWN, KC], [1, T5]])), writes=[("mtl", t % 2)])

        def load_x(g):
            S.add("sp", lambda e, g=g: e.dma_start(out=xl[g % 3][:], in_=x_own.ap()[g * 128:(g + 1) * 128, :]), writes=[("xl", g % 3)])

        def comp5b(g):
            t = g // BPB
            b_ = g % BPB
            ms_ = t % 2
            sl = g % 3
            for n in range(4):
                pi = pcnt[0] % 4
                pcnt[0] += 1
                for kc in range(KC):
                    S.add("pe", lambda e, pi=pi, kc=kc, n=n: e.matmul(pg5[pi][:, :], lhsT=mtl[ms_][:, kc, b_ * 128:(b_ + 1) * 128], rhs=wo[:, kc, n * 512:(n + 1) * 512],
                                                                   start=(kc == 0), stop=(kc == KC - 1)), reads=wkeys + [("mtl", ms_)], writes=[("pg5b", pi)])
                S.add("dve", lambda e, pi=pi, n=n: e.tensor_tensor(out=xl[sl][:, n * 512:(n + 1) * 512], in0=pg5[pi][:, :], in1=xl[sl][:, n * 512:(n + 1) * 512], op=ALU.add),
                      reads=[("pg5b", pi), ("xl", sl)], writes=[("xl", sl)])
            S.add("sp", lambda e, g=g: e.dma_start(out=X1.ap()[g * 128:(g + 1) * 128, :], in_=xl[sl][:]), reads=[("xl", sl)], writes=[("X1", g)])

        for t in range(min(2, NT5B)):
            load_mt(t)
        for g in range(min(2, NG)):
            load_x(g)
        for g in range(NG):
            if g + 2 < NG:
                load_x(g + 2)
            comp5b(g)
            if (g + 1) % BPB == 0 and g // BPB + 2 < NT5B:
                load_mt(g // BPB + 2)
        S.barrier()

    SC_C = 128 ** -0.5
    T6 = min(512, TOWN)
    with ExitStack() as es:
        gcr = SB(es, "gcr", [128, D], F32)
        bcast_load(gcr, g_cross, D, "gcr")
        wq = SB(es, "wq", [128, KC, 512], BF16)
        S.add("pq", lambda e: e.dma_start(out=wq[:], in_=WCQ_B.ap().rearrange("(k p) n -> p k n", p=128)), writes=["wq"])
        wco = SB(es, "wco", [128, 4, D], BF16)
        S.add("pq", lambda e: e.dma_start(out=wco[:], in_=WCO_B.ap().rearrange("(k p) n -> p k n", p=128)), writes=["wco"])
        BP6 = T6 // 128
        x1l = [SB(es, f"x1l{i}", [128, D], F32) for i in range(2 * BP6)]
        hb6 = [SB(es, f"hb6{i}", [128, D], BF16) for i in range(BP6)]
        junk = SB(es, "junk6", [128, D], BF16)
        ss6 = [SB(es, f"ss6{i}", [128, 2], F32) for i in range(BP6)]
        H2Ts = [SB(es, f"H2T{i}", [128, KC, T6], BF16) for i in range(2)]
        qc = SB(es, "qc", [128, 4, T6], BF16)
        pc = [SB(es, f"pc{i}", [128, 2, T6], BF16) for i in range(2)]
        rl = [SB(es, f"rl{i}", [128, T6], F32) for i in range(2)]
        oc = SB(es, "oc", [128, 4, T6], BF16)
        ptr = [PS(es, "ptr6_0", [128, D], BF16)]
        pq6 = PS(es, "pq6", [128, 512])
        ps6 = PS(es, "ps6", [128, 2, 512])
        po6 = PS(es, "po6", [128, 512])
        pl6 = PS(es, "pl6", [128, 512])
        NT6 = TOWN // T6
        obanks = [(pq6[:, :], "pq6"), (po6[:, :], "po6"), (pl6[:, :], "pl6"), (ps6[:, 0, :], ("ps6", 0)), (ps6[:, 1, :], ("ps6", 1))]
        ocnt = [0]

        def loads6(t):
            for b_ in range(BP6):
                xs_ = (t % 2) * BP6 + b_
                r0 = t * T6 + b_ * 128
                S.add("sp", lambda e, xs_=xs_, r0=r0: e.dma_start(out=x1l[xs_][:], in_=X1.ap()[r0:r0 + 128, :]), writes=[("x1l", xs_)])

        def rms6(t):
            for b_ in range(BP6):
                xs_ = (t % 2) * BP6 + b_
                S.add("act", lambda e, xs_=xs_, b_=b_: e.activation(out=junk[:], in_=x1l[xs_][:], func=AF.Square, accum_out=ss6[b_][:, 0:1]),
                      reads=[("x1l", xs_)], writes=[("ss6", b_)])
                S.add("act", lambda e, b_=b_: e.activation(out=ss6[b_][:, 1:2], in_=ss6[b_][:, 0:1], func=AF.Ln, bias=epsc[:, 0:1], scale=1.0 / D),
                      reads=[("ss6", b_), "epsc"], writes=[("ss6", b_)])
                S.add("act", lambda e, b_=b_: e.activation(out=ss6[b_][:, 1:2], in_=ss6[b_][:, 1:2], func=AF.Exp, scale=-0.5),
                      reads=[("ss6", b_)], writes=[("ss6", b_)])
                S.add("dve", lambda e, xs_=xs_, b_=b_: e.scalar_tensor_tensor(out=hb6[b_][:], in0=x1l[xs_][:], scalar=ss6[b_][:, 1:2], in1=gcr[:], op0=ALU.mult, op1=ALU.mult),
                      reads=[("x1l", xs_), ("ss6", b_), "gcr"], writes=[("hb6", b_)])

        def trans6(t, b_):
            transpose_block(hb6[b_], ("hb6", b_), ptr, H2Ts[t % 2], ("H2T", t % 2), b_ * 128)

        def back6(t, nxt):
            H2T = H2Ts[t % 2]
            hk = ("H2T", t % 2)
            if nxt is not None:
                rms6(nxt)
            for hd in range(4):
                for kc in range(KC):
                    S.add("pe", lambda e, hd=hd, kc=kc: e.matmul(pq6[:, 0:T6], lhsT=wq[:, kc, hd * 128:(hd + 1) * 128], rhs=H2T[:, kc, :],
                                                               start=(kc == 0), stop=(kc == KC - 1)), reads=["wq", hk], writes=["pq6"])
                S.add("act", lambda e, hd=hd: e.mul(out=qc[:, hd, :], in_=pq6[:, 0:T6], mul=SC_C), reads=["pq6"], writes=[("qc", hd)])

            def s_mm(j):
                hd, mb = j // 2, j % 2
                S.add("pe", lambda e, hd=hd, mb=mb: e.matmul(ps6[:, mb, 0:T6], lhsT=kct[:, hd, mb * 128:(mb + 1) * 128], rhs=qc[:, hd, :], start=True, stop=True),
                      reads=["kct", ("qc", hd)], writes=[("ps6", mb)])
                S.add("act", lambda e, hd=hd, mb=mb: e.activation(out=pc[hd % 2][:, mb, :], in_=ps6[:, mb, 0:T6], func=AF.Exp), reads=[("ps6", mb)], writes=[("pc", hd % 2, mb)])

            def pv_mm(j):
                hd, mb = j // 2, j % 2
                pcs = hd % 2
                S.add("pe", lambda e, hd=hd, mb=mb, pcs=pcs: e.matmul(po6[:, 0:T6], lhsT=vc[:, mb, hd * 128:(hd + 1) * 128], rhs=pc[pcs][:, mb, :], start=(mb == 0), stop=(mb == 1)),
                      reads=["vc", ("pc", pcs, mb)], writes=["po6"])
                S.add("pe", lambda e, mb=mb, pcs=pcs: e.matmul(pl6[:, 0:T6], lhsT=onesb[:], rhs=pc[pcs][:, mb, :], start=(mb == 0), stop=(mb == 1)),
                      reads=["onesb", ("pc", pcs, mb)], writes=["pl6"])
                if mb == 1:
                    S.add("act", lambda e, pcs=pcs: e.activation(out=rl[pcs][:], in_=pl6[:, 0:T6], func=AF.Ln), reads=["pl6"], writes=[("rl", pcs)])
                    S.add("act", lambda e, pcs=pcs: e.activation(out=rl[pcs][:], in_=rl[pcs][:], func=AF.Exp, scale=-1.0), reads=[("rl", pcs)], writes=[("rl", pcs)])
                    S.add("dve", lambda e, hd=hd, pcs=pcs: e.tensor_tensor(out=oc[:, hd, :], in0=po6[:, 0:T6], in1=rl[pcs][:], op=ALU.mult), reads=["po6", ("rl", pcs)], writes=[("oc", hd)])

            s_mm(0)
            for j in range(8):
                if j + 1 < 8:
                    s_mm(j + 1)
                pv_mm(j)
            for b_ in range(BP6):
                xs_ = (t % 2) * BP6 + b_
                r0 = t * T6 + b_ * 128
                for n in range(4):
                    bank, bkey = obanks[ocnt[0] % len(obanks)]
                    ocnt[0] += 1
                    for hd in range(4):
                        S.add("pe", lambda e, hd=hd, n=n, b_=b_, bank=bank: e.matmul(bank, lhsT=oc[:, hd, b_ * 128:(b_ + 1) * 128], rhs=wco[:, hd, n * 512:(n + 1) * 512],
                                                                                   start=(hd == 0), stop=(hd == 3)), reads=["wco"] + [("oc", i) for i in range(4)], writes=[bkey])
                    S.add("dve", lambda e, n=n, xs_=xs_, bank=bank: e.tensor_tensor(out=x1l[xs_][:, n * 512:(n + 1) * 512], in0=bank, in1=x1l[xs_][:, n * 512:(n + 1) * 512], op=ALU.add),
                          reads=[bkey, ("x1l", xs_)], writes=[("x1l", xs_)])
                S.add("sp", lambda e, xs_=xs_, r0=r0: e.dma_start(out=X2.ap()[r0:r0 + 128, :], in_=x1l[xs_][:]), reads=[("x1l", xs_)], writes=[("X2", r0 // 128)])
                if nxt is not None:
                    trans6(nxt, b_)

        for t in range(min(2, NT6)):
            loads6(t)
        rms6(0)
        for b_ in range(BP6):
            trans6(0, b_)
        for t in range(NT6):
            back6(t, t + 1 if t + 1 < NT6 else None)
            if t + 2 < NT6:
                loads6(t + 2)
        S.barrier()

    FW = 256
    with ExitStack() as es:
        gff = SB(es, "gff", [128, D], F32)
        bcast_load(gff, g_ffn, D, "gff")
        H3T = SB(es, "H3T", [128, KC, T7], BF16)
        x2l = [SB(es, f"x2l{i}", [128, D], F32) for i in range(2)]
        hb7 = [SB(es, f"hb7{i}", [128, D], BF16) for i in range(2)]
        junk = SB(es, "junk7", [128, D], BF16)
        ss7 = [SB(es, f"ss7{i}", [128, 1], F32) for i in range(2)]
        rs7 = [SB(es, f"rs7{i}", [128, 1], F32) for i in range(2)]
        wg = [SB(es, f"wg{i}", [128, KC, FW], BF16) for i in range(2)]
        wu = [SB(es, f"wu{i}", [128, KC, FW], BF16) for i in range(2)]
        sg = [SB(es, f"sg{i}", [128, 512], F32) for i in range(2)]
        ast = [SB(es, f"ast{i}", [128, FW // 128, T7], BF16) for i in range(2)]
        ptr = [PS(es, f"ptr7_{i}", [128, D], BF16) for i in range(2)]
        pgt = [PS(es, f"pgt{i}", [128, 512]) for i in range(2)]
        put = [PS(es, f"put{i}", [128, 512]) for i in range(2)]
        w_fi_v = WFI_B.ap().rearrange("(k p) n -> p k n", p=128)
        xc = 0
        wc = 0
        pc_ = 0
        TW = min(512, T7)
        for t in range(TOWN // T7):
            for b in range(T7 // 128):
                sl = xc % 2
                xc += 1
                r0 = t * T7 + b * 128
                S.add("sp", lambda e, sl=sl, r0=r0: e.dma_start(out=x2l[sl][:], in_=X2.ap()[r0:r0 + 128, :]), writes=[("x2l", sl)])
                rms_block(x2l[sl][:], ("x2l", sl), gff, "gff", hb7[sl], ("hb7", sl), junk, ss7[sl], rs7[sl], ("p7", sl))
                transpose_block(hb7[sl], ("hb7", sl), ptr, H3T, "H3T", b * 128)
            for fc in range(DFF // FW):
                ws = wc % 2
                wc += 1
                S.add("pq", lambda e, ws=ws, fc=fc: e.dma_start(out=wg[ws][:], in_=w_fi_v[:, :, fc * FW:(fc + 1) * FW]), writes=[("wg", ws)])
                S.add("pq", lambda e, ws=ws, fc=fc: e.dma_start(out=wu[ws][:], in_=w_fi_v[:, :, DFF + fc * FW:DFF + (fc + 1) * FW]), writes=[("wu", ws)])
                for fb in range(FW // 128):
                    for tt in range(T7 // TW):
                        pi = pc_ % 2
                        pc_ += 1
                        for kc in range(KC):
                            S.add("pe", lambda e, pi=pi, ws=ws, kc=kc, fb=fb, tt=tt: e.matmul(pgt[pi][:, 0:TW], lhsT=wg[ws][:, kc, fb * 128:(fb + 1) * 128], rhs=H3T[:, kc, tt * TW:(tt + 1) * TW],
                                                                                         start=(kc == 0), stop=(kc == KC - 1)), reads=[("wg", ws), "H3T"], writes=[("pgt", pi)])
                        for kc in range(KC):
                            S.add("pe", lambda e, pi=pi, ws=ws, kc=kc, fb=fb, tt=tt: e.matmul(put[pi][:, 0:TW], lhsT=wu[ws][:, kc, fb * 128:(fb + 1) * 128], rhs=H3T[:, kc, tt * TW:(tt + 1) * TW],
                                                                                         start=(kc == 0), stop=(kc == KC - 1)), reads=[("wu", ws), "H3T"], writes=[("put", pi)])
                        S.add("act", lambda e, pi=pi: e.activation(out=sg[pi][:, 0:TW], in_=pgt[pi][:, 0:TW], func=AF.Silu), reads=[("pgt", pi)], writes=[("sg", pi)])
                        S.add("dve", lambda e, pi=pi, ws=ws, fb=fb, tt=tt: e.tensor_tensor(out=ast[ws][:, fb, tt * TW:(tt + 1) * TW], in0=put[pi][:, 0:TW], in1=sg[pi][:, 0:TW], op=ALU.mult),
                              reads=[("put", pi), ("sg", pi)], writes=[("ast", ws)])
                S.add("sp", lambda e, ws=ws, fc=fc, t=t: e.dma_start(out=dap(ACTD, fc * (FW // 128) * 128 * TOWN + t * T7, [[TOWN, 128], [128 * TOWN, FW // 128], [1, T7]]), in_=ast[ws][:]),
                      reads=[("ast", ws)], writes=[("ACTD", fc, t)])
        S.barrier()

    with ExitStack() as es:
        gfi = SB(es, "gfi", [128, D], F32)
        bcast_load(gfi, g_fin, D, "gfi")
        ACTT = SB(es, "ACTT", [128, FC, T8], BF16)
        wfo = [SB(es, f"wfo{i}", [128, FC, FW], BF16) for i in range(2)]
        xr = [SB(es, f"xr{i}", [128, FW], F32) for i in range(8)]
        x2l = [SB(es, f"x8l{i}", [128, D], F32) for i in range(2)]
        junk = SB(es, "junk8", [128, D], BF16)
        ss7 = [SB(es, f"ss8{i}", [128, 2], F32) for i in range(2)]
        pgt = [PS(es, f"pg8{i}", [128, 512]) for i in range(4)]
        w_fo_v = WFO_B.ap().rearrange("(k p) n -> p k n", p=128)
        NT8 = TOWN // T8
        NN = D // FW
        BP8 = T8 // 128
        items = [(t, n, b_) for t in range(NT8) for n in range(NN) for b_ in range(BP8)]
        xcnt = [0]

        def load_actt(t):
            for g4 in range(4):
                S.add("sp", lambda e, g4=g4, t=t: e.dma_start(out=ACTT[:, g4 * 11:(g4 + 1) * 11, :],
                                                               in_=dap(ACTD, g4 * 11 * 128 * TOWN + t * T8, [[TOWN, 128], [128 * TOWN, 11], [1, T8]])),
                      writes=[("ACTT", g4)])

        def load_wfo(ci):
            n = ci % NN
            S.add("pq", lambda e, ci=ci, n=n: e.dma_start(out=wfo[ci % 2][:], in_=w_fo_v[:, :, n * FW:(n + 1) * FW]), writes=[("wfo", ci % 2)])

        def load_xr(k):
            t, n, b_ = items[k]
            r0 = t * T8 + b_ * 128
            S.add("sp", lambda e, k=k, n=n, r0=r0: e.dma_start(out=xr[k % 8][:], in_=dap(X2, r0 * D + n * FW, [[D, 128], [1, FW]])), writes=[("xr", k % 8)])

        def comp8(k):
            t, n, b_ = items[k]
            ci = t * NN + n
            pi = k % 4
            r0 = t * T8 + b_ * 128
            for kc in range(FC):
                S.add("pe", lambda e, pi=pi, ci=ci, kc=kc, b_=b_: e.matmul(pgt[pi][:, 0:FW], lhsT=ACTT[:, kc, b_ * 128:(b_ + 1) * 128], rhs=wfo[ci % 2][:, kc, :],
                                                                       start=(kc == 0), stop=(kc == FC - 1)), reads=[("wfo", ci % 2), ("ACTT", kc // 11)], writes=[("pg8", pi)])
            xi = k % 8
            S.add("dve", lambda e, pi=pi, xi=xi: e.tensor_tensor(out=xr[xi][:], in0=pgt[pi][:, 0:FW], in1=xr[xi][:], op=ALU.add), reads=[("pg8", pi), ("xr", xi)], writes=[("xr", xi)])
            S.add("sp", lambda e, xi=xi, n=n, r0=r0: e.dma_start(out=dap(X2, r0 * D + n * FW, [[D, 128], [1, FW]]), in_=xr[xi][:]),
                  reads=[("xr", xi)], writes=[("X3", r0 // 128, n)])

        def final8(t, blocks=None):
            for b_ in (range(BP8) if blocks is None else blocks):
                sl = xcnt[0] % 2
                xcnt[0] += 1
                r0 = t * T8 + b_ * 128
                S.add("sp", lambda e, sl=sl, r0=r0: e.dma_start(out=x2l[sl][:], in_=X2.ap()[r0:r0 + 128, :]),
                      reads=[("X3", r0 // 128, n) for n in range(NN)], writes=[("x8l", sl)])
                S.add("act", lambda e, sl=sl: e.activation(out=junk[:], in_=x2l[sl][:], func=AF.Square, accum_out=ss7[sl][:, 0:1]),
                      reads=[("x8l", sl)], writes=[("ss8", sl)])
                S.add("act", lambda e, sl=sl: e.activation(out=ss7[sl][:, 1:2], in_=ss7[sl][:, 0:1], func=AF.Ln, bias=epsc[:, 0:1], scale=1.0 / D),
                      reads=[("ss8", sl), "epsc"], writes=[("ss8", sl)])
                S.add("act", lambda e, sl=sl: e.activation(out=ss7[sl][:, 1:2], in_=ss7[sl][:, 1:2], func=AF.Exp, scale=-0.5),
                      reads=[("ss8", sl)], writes=[("ss8", sl)])
                S.add("dve", lambda e, sl=sl: e.scalar_tensor_tensor(out=x2l[sl][:], in0=x2l[sl][:], scalar=ss7[sl][:, 1:2], in1=gfi[:], op0=ALU.mult, op1=ALU.mult),
                      reads=[("x8l", sl), ("ss8", sl), "gfi"], writes=[("x8l", sl)])
                S.add("sp", lambda e, sl=sl, r0=r0: e.dma_start(out=y_out.ap()[r0:r0 + 128, :], in_=x2l[sl][:]), reads=[("x8l", sl)], writes=[("y", r0 // 128)])

        load_actt(0)
        load_wfo(0)
        for k in range(min(6, len(items))):
            load_xr(k)
        pend_final = None
        pend_blocks = []
        for k, (t, n, b_) in enumerate(items):
            ci = t * NN + n
            if b_ == 0:
                if ci + 1 < NT8 * NN:
                    load_wfo(ci + 1)
            if b_ == BP8 // 2 and pend_final is not None and pend_blocks:
                nb_ = -(-BP8 // NN)
                final8(pend_final, pend_blocks[:nb_])
                pend_blocks = pend_blocks[nb_:]
            if k + 6 < len(items):
                load_xr(k + 6)
            comp8(k)
            if n == NN - 1 and b_ == BP8 - 1:
                if pend_final is not None and pend_blocks:
                    final8(pend_final, pend_blocks)
                if t + 1 < NT8:
                    load_actt(t + 1)
                pend_final = t
                pend_blocks = list(range(BP8))
        if pend_final is not None and pend_blocks:
            final8(pend_final, pend_blocks)
        S.barrier()

    print("ops", len(S.ops), "sbuf_rem", nc.sbuf_bytes_remaining)
    S.build(sem)
    top.close()
    return nc, LU, U0, QC


def _t5_bucket_np(rel):
    rel = np.asarray(rel, np.int32)
    half = 16
    max_exact = 8
    n = np.abs(rel)
    nf = np.maximum(n, 1).astype(np.float32)
    large = max_exact + (np.log(nf / np.float32(max_exact)) / np.float32(math.log(128 / max_exact))
                         * np.float32(half - max_exact)).astype(np.int32)
    large = np.minimum(large, half - 1)
    return np.where(rel > 0, half, 0) + np.where(n < max_exact, n, large)


def _structural(kind, LU, U0):
    j = np.arange(LU)
    bk = _t5_bucket_np(U0 - j)
    near = np.zeros((33, LU), np.float32)
    near[bk, j] = 1.0

    def const(b):
        a = np.zeros((33, LU), np.float32)
        a[b, :] = 1.0
        return a

    def selv(b):
        a = np.zeros((33, 128), np.float32)
        a[b, :] = 1.0
        return a
    if kind == "p0":
        hi, lo, oth = near, const(31), 31
    elif kind == "p1":
        hi, lo, oth = const(15), near, 15
    else:
        hi, lo, oth = const(32), const(32), 32
    oh_all = np.concatenate([near, hi, lo], axis=1)
    sel_all = np.concatenate([selv(15), selv(31), selv(oth)], axis=1)
    return np.ascontiguousarray(oh_all), np.ascontiguousarray(sel_all)


_WNAMES = ["rel_bias_table", "norm_mix_g", "w_in", "ln_v_g", "ln_v_b", "w_spatial", "b_spatial", "lambda_q1", "lambda_k1",
           "lambda_q2", "lambda_k2", "subln_g", "w_proj_a", "w_proj_b", "w_out", "norm_cross_g", "norm_mem_g", "w_cq", "w_ck",
           "w_cv", "w_co", "norm_ffn_g", "w_ffn_in", "w_ffn_out", "norm_final_g"]


def run_layer(inputs, n_prompt_cores=4):
    xp = np.asarray(inputs["x_prompt"], np.float32)
    xs = np.asarray(inputs["x_sample"], np.float32)
    mp = np.asarray(inputs["mem_prompt"], np.float32)
    msm = np.asarray(inputs["mem_sample"], np.float32)
    B, SEQ, _ = xp.shape
    DB, DSEQ, _ = xs.shape
    TOWN = SEQ // 2
    assert DSEQ == TOWN and B * 2 + DB == 8
    nc, LU, U0, QC = build_program(TOWN)
    wts = {}
    for k in _WNAMES:
        a = np.asarray(inputs[k], np.float32)
        if k != "norm_final_g" and k != "rel_bias_table":
            a = a[0]
        wts[k] = np.ascontiguousarray(a)
    roles = []
    pi_, si_ = 0, 0
    for c in range(8):
        if (c % 4) < 2 and pi_ < 2 * B:
            roles.append(("p", pi_)); pi_ += 1
        elif si_ < DB:
            roles.append(("s", si_)); si_ += 1
        else:
            roles.append(("p", pi_)); pi_ += 1
    zeros_oth = np.zeros((TOWN, xs.shape[2]), np.float32)
    in_maps = []
    for c in range(8):
        m = dict(wts)
        kind, idx = roles[c]
        if kind == "p":
            b, half = idx // 2, idx % 2
            m["x_own"] = np.ascontiguousarray(xp[b, half * TOWN:(half + 1) * TOWN])
            m["x_oth"] = np.ascontiguousarray(xp[b, (1 - half) * TOWN:(2 - half) * TOWN])
            m["mem"] = np.ascontiguousarray(mp[b])
            oh, sel = _structural("p0" if half == 0 else "p1", LU, U0)
        else:
            b = idx
            m["x_own"] = np.ascontiguousarray(xs[b])
            m["x_oth"] = zeros_oth
            m["mem"] = np.ascontiguousarray(msm[b])
            oh, sel = _structural("s", LU, U0)
        m["oh_all"] = oh
        m["sel_all"] = sel
        in_maps.append(m)
    res = run_bass_kernel_spmd(nc, in_maps, core_ids=list(range(8)))
    yp = np.empty_like(xp)
    ys = np.empty_like(xs)
    for c in range(8):
        y = np.asarray(res.results[c]["y"], np.float32)
        kind, idx = roles[c]
        if kind == "p":
            b, half = idx // 2, idx % 2
            yp[b, half * TOWN:(half + 1) * TOWN] = y
        else:
            ys[idx] = y
    return yp, ys


def kernel(**inputs):
    return run_layer(inputs)
```
